# Optimizing a Trainium2 kernel written in Bass

```python
import math
import jax, jax.numpy as jnp
from jax import lax
import numpy as np

D_MODEL = 1024
BATCH = 4
SEQ = 4096
DEPTH = 1

SSD_EXPAND = 2
D_INNER = SSD_EXPAND * D_MODEL
SSD_HEAD_DIM = 64
SSD_HEADS = D_INNER // SSD_HEAD_DIM
SSD_GROUPS = 4
SSD_STATE = 128
SSD_CONV = 4
SSD_CHUNK = 256
CONV_DIM = D_INNER + 2 * SSD_GROUPS * SSD_STATE

DIFF_HEADS = 8
DIFF_HEAD_DIM = 64
DIFF_QK = DIFF_HEADS * 2 * DIFF_HEAD_DIM
DIFF_V = DIFF_HEADS * 2 * DIFF_HEAD_DIM
Q_BLOCK = 128
ROPE_THETA = 10000.0

D_FF = -(-8 * D_MODEL // (3 * 256)) * 256

N_BRANCH = 2
EPS = 1e-6

IN_SIZES = (D_INNER, CONV_DIM, SSD_HEADS, DIFF_QK, DIFF_QK, DIFF_V, N_BRANCH * D_MODEL)
IN_SPLITS = tuple(int(v) for v in np.cumsum(IN_SIZES)[:-1])
D_IN_PROJ = sum(IN_SIZES)

kernel_name = 'hybrid_ssd_diffattn_gated_block'


def rmsnorm(x, w):
    xf = x.astype(jnp.float32)
    y = xf * lax.rsqrt(jnp.mean(xf * xf, axis=-1, keepdims=True) + EPS)
    return (y * w.astype(jnp.float32)).astype(x.dtype)


def rope_tables(positions):
    inv_freq = 1.0 / (ROPE_THETA ** (jnp.arange(0, DIFF_HEAD_DIM, 2, dtype=jnp.float32) / DIFF_HEAD_DIM))
    ang = positions.astype(jnp.float32)[..., None] * inv_freq
    return jnp.cos(ang), jnp.sin(ang)


def apply_rope(x, cos, sin):
    cos = cos[:, :, None, None, :]
    sin = sin[:, :, None, None, :]
    xf = x.astype(jnp.float32)
    x1, x2 = jnp.split(xf, 2, axis=-1)
    out = jnp.concatenate([x1 * cos - x2 * sin, x1 * sin + x2 * cos], axis=-1)
    return out.astype(x.dtype)


def causal_depthwise_conv(u, w, b):
    rhs = jnp.transpose(w)[:, None, :].astype(u.dtype)
    out = lax.conv_general_dilated(u, rhs, window_strides=(1,), padding=[(SSD_CONV - 1, 0)],
                                   dimension_numbers=('NWC', 'WIO', 'NWC'),
                                   feature_group_count=u.shape[-1])
    return out + b.astype(u.dtype)


def ssd_chunked_scan(x, dt, A, Bm, Cm):
    b, s, h, p = x.shape
    g, n = Bm.shape[2], Bm.shape[3]
    hg = h // g
    L = SSD_CHUNK
    pad = (-s) % L
    xf = x.astype(jnp.float32) * dt[..., None]
    a = dt * A
    Bf = Bm.astype(jnp.float32)
    Cf = Cm.astype(jnp.float32)
    if pad:
        xf = jnp.pad(xf, ((0, 0), (0, pad), (0, 0), (0, 0)))
        a = jnp.pad(a, ((0, 0), (0, pad), (0, 0)))
        Bf = jnp.pad(Bf, ((0, 0), (0, pad), (0, 0), (0, 0)))
        Cf = jnp.pad(Cf, ((0, 0), (0, pad), (0, 0), (0, 0)))
    nc = (s + pad) // L
    xc = xf.reshape(b, nc, L, g, hg, p)
    a_cs = jnp.cumsum(a.reshape(b, nc, L, g, hg), axis=2)
    Bc = Bf.reshape(b, nc, L, g, n)
    Cc = Cf.reshape(b, nc, L, g, n)
    mask = jnp.tril(jnp.ones((L, L), dtype=bool))[:, :, None, None]
    seg = a_cs[:, :, :, None] - a_cs[:, :, None, :]
    decay = jnp.exp(jnp.where(mask, seg, -jnp.inf))
    scores = jnp.einsum('bclgn,bcsgn->bclsg', Cc, Bc)
    y_diag = jnp.einsum('bclsg,bclsgh,bcsghp->bclghp', scores, decay, xc)
    decay_to_end = jnp.exp(a_cs[:, :, -1:] - a_cs)
    states = jnp.einsum('bclgn,bclgh,bclghp->bcghpn', Bc, decay_to_end, xc)
    chunk_decay = jnp.exp(a_cs[:, :, -1])

    def step(carry, inp):
        st, dec = inp
        return carry * dec[..., None, None] + st, carry

    init = jnp.zeros((b, g, hg, p, n), jnp.float32)
    _, prev = lax.scan(step, init, (jnp.moveaxis(states, 1, 0), jnp.moveaxis(chunk_decay, 1, 0)))
    prev = jnp.moveaxis(prev, 0, 1)
    y_off = jnp.einsum('bclgn,bcghpn,bclgh->bclghp', Cc, prev, jnp.exp(a_cs))
    y = (y_diag + y_off).reshape(b, nc * L, h, p)
    return y[:, :s]


def diff_attention(q, k, v, lam):
    b, s, h, _, d = q.shape
    nb = s // Q_BLOCK
    qb = q.reshape(b, nb, Q_BLOCK, h, 2, d).transpose(1, 0, 3, 4, 2, 5)
    kt = k.transpose(0, 2, 3, 1, 4)
    vt = v.transpose(0, 2, 1, 3)
    key_idx = jnp.arange(s)
    scale = d ** -0.5

    def block(args):
        qblk, i = args
        sc = jnp.einsum('bhcqd,bhckd->bhcqk', qblk, kt).astype(jnp.float32) * scale
        q_idx = i * Q_BLOCK + jnp.arange(Q_BLOCK)
        causal = key_idx[None, :] <= q_idx[:, None]
        pr = jax.nn.softmax(jnp.where(causal, sc, -jnp.inf), axis=-1)
        attn = pr[:, :, 0] - lam * pr[:, :, 1]
        return jnp.einsum('bhqk,bhkv->bhqv', attn.astype(v.dtype), vt)

    out = lax.map(block, (qb, jnp.arange(nb)))
    return out.transpose(1, 0, 3, 2, 4).reshape(b, s, h, 2 * d)


def hybrid_mixer(h, cos, sin, lambda_init, w_in, conv_w, conv_b, dt_bias, a_log, d_skip, ssd_norm, w_o_ssd,
                 lambda_q1, lambda_k1, lambda_q2, lambda_k2, subln, w_o_attn, w_out):
    b, s, _ = h.shape
    proj = h @ w_in
    z, xbc, dt_raw, q, k, v, gate_logits = jnp.split(proj, IN_SPLITS, axis=-1)

    xbc = jax.nn.silu(causal_depthwise_conv(xbc, conv_w, conv_b))
    xs, Bm, Cm = jnp.split(xbc, [D_INNER, D_INNER + SSD_GROUPS * SSD_STATE], axis=-1)
    xs = xs.reshape(b, s, SSD_HEADS, SSD_HEAD_DIM)
    Bm = Bm.reshape(b, s, SSD_GROUPS, SSD_STATE)
    Cm = Cm.reshape(b, s, SSD_GROUPS, SSD_STATE)
    dt = jax.nn.softplus(dt_raw.astype(jnp.float32) + dt_bias.astype(jnp.float32))
    A = -jnp.exp(a_log.astype(jnp.float32))
    y = ssd_chunked_scan(xs, dt, A, Bm, Cm) + xs.astype(jnp.float32) * d_skip.astype(jnp.float32)[:, None]
    y = y.reshape(b, s, D_INNER) * jax.nn.silu(z.astype(jnp.float32))
    y = rmsnorm(y.reshape(b, s, SSD_GROUPS, D_INNER // SSD_GROUPS), ssd_norm.reshape(SSD_GROUPS, -1))
    y_ssd = y.reshape(b, s, D_INNER).astype(h.dtype) @ w_o_ssd

    q = apply_rope(q.reshape(b, s, DIFF_HEADS, 2, DIFF_HEAD_DIM), cos, sin)
    k = apply_rope(k.reshape(b, s, DIFF_HEADS, 2, DIFF_HEAD_DIM), cos, sin)
    v = v.reshape(b, s, DIFF_HEADS, 2 * DIFF_HEAD_DIM)
    lam = (jnp.exp(jnp.sum(lambda_q1.astype(jnp.float32) * lambda_k1.astype(jnp.float32)))
           - jnp.exp(jnp.sum(lambda_q2.astype(jnp.float32) * lambda_k2.astype(jnp.float32))) + lambda_init)
    o = diff_attention(q, k, v, lam)
    o = rmsnorm(o, subln) * (1.0 - lambda_init)
    y_attn = o.reshape(b, s, DIFF_V) @ w_o_attn

    g_ssd, g_attn = jnp.split(jax.nn.sigmoid(gate_logits), N_BRANCH, axis=-1)
    merged = g_ssd * y_ssd + g_attn * y_attn
    return merged @ w_out


def setup_inputs(seed: int = 0) -> dict:
    key = jax.random.key(seed)
    ks = jax.random.split(key, 32)
    f32 = jnp.float32
    nrm = lambda k, shape, fan: jax.random.normal(k, shape, f32) * fan ** -0.5
    gain = lambda k, shape: 1.0 + 0.02 * jax.random.normal(k, shape, f32)
    x = jax.random.normal(ks[0], (BATCH, SEQ, D_MODEL), f32)
    c = jax.random.normal(ks[1], (BATCH, D_MODEL), f32)
    offset = jax.random.randint(ks[2], (BATCH, 1), 0, 1024, dtype=jnp.int32)
    positions = offset + jnp.arange(SEQ, dtype=jnp.int32)[None, :]
    dt_init = jnp.exp(jax.random.uniform(ks[3], (DEPTH, SSD_HEADS), f32, math.log(1e-3), math.log(1e-1)))
    dt_bias = dt_init + jnp.log(-jnp.expm1(-dt_init))
    a_log = jnp.log(jax.random.uniform(ks[4], (DEPTH, SSD_HEADS), f32, 1.0, 16.0))
    return {
        'x': x, 'c': c, 'positions': positions,
        'w_ada': 0.5 * nrm(ks[5], (DEPTH, D_MODEL, 6 * D_MODEL), D_MODEL),
        'b_ada': 0.01 * jax.random.normal(ks[6], (DEPTH, 6 * D_MODEL), f32),
        'norm_pre_mix': gain(ks[7], (DEPTH, D_MODEL)),
        'norm_post_mix': gain(ks[8], (DEPTH, D_MODEL)),
        'norm_pre_ffn': gain(ks[9], (DEPTH, D_MODEL)),
        'norm_post_ffn': gain(ks[10], (DEPTH, D_MODEL)),
        'w_in': nrm(ks[11], (DEPTH, D_MODEL, D_IN_PROJ), D_MODEL),
        'conv_w': nrm(ks[12], (DEPTH, CONV_DIM, SSD_CONV), SSD_CONV),
        'conv_b': 0.01 * jax.random.normal(ks[13], (DEPTH, CONV_DIM), f32),
        'dt_bias': dt_bias,
        'a_log': a_log,
        'd_skip': gain(ks[14], (DEPTH, SSD_HEADS)),
        'ssd_norm': gain(ks[15], (DEPTH, D_INNER)),
        'w_o_ssd': nrm(ks[16], (DEPTH, D_INNER, D_MODEL), D_INNER),
        'lambda_q1': 0.1 * jax.random.normal(ks[17], (DEPTH, DIFF_HEAD_DIM), f32),
        'lambda_k1': 0.1 * jax.random.normal(ks[18], (DEPTH, DIFF_HEAD_DIM), f32),
        'lambda_q2': 0.1 * jax.random.normal(ks[19], (DEPTH, DIFF_HEAD_DIM), f32),
        'lambda_k2': 0.1 * jax.random.normal(ks[20], (DEPTH, DIFF_HEAD_DIM), f32),
        'subln': gain(ks[21], (DEPTH, 2 * DIFF_HEAD_DIM)),
        'w_o_attn': nrm(ks[22], (DEPTH, DIFF_V, D_MODEL), DIFF_V),
        'w_out': nrm(ks[23], (DEPTH, D_MODEL, D_MODEL), D_MODEL),
        'w_gate': nrm(ks[24], (DEPTH, D_MODEL, D_FF), D_MODEL),
        'w_up': nrm(ks[25], (DEPTH, D_MODEL, D_FF), D_MODEL),
        'w_down': nrm(ks[26], (DEPTH, D_FF, D_MODEL), D_FF),
    }


def reference(x, c, positions, w_ada, b_ada, norm_pre_mix, norm_post_mix, norm_pre_ffn, norm_post_ffn,
              w_in, conv_w, conv_b, dt_bias, a_log, d_skip, ssd_norm, w_o_ssd,
              lambda_q1, lambda_k1, lambda_q2, lambda_k2, subln, w_o_attn, w_out,
              w_gate, w_up, w_down):
    cos, sin = rope_tables(positions)
    cond = jax.nn.silu(c)
    for layer in range(DEPTH):
        lambda_init = 0.8 - 0.6 * math.exp(-0.3 * layer)
        mod = (cond @ w_ada[layer] + b_ada[layer])[:, None, :]
        sh1, sc1, g1, sh2, sc2, g2 = jnp.split(mod, 6, axis=-1)
        h = rmsnorm(x, norm_pre_mix[layer]) * (1.0 + sc1) + sh1
        mix = hybrid_mixer(h, cos, sin, lambda_init, w_in[layer], conv_w[layer], conv_b[layer], dt_bias[layer],
                           a_log[layer], d_skip[layer], ssd_norm[layer], w_o_ssd[layer],
                           lambda_q1[layer], lambda_k1[layer], lambda_q2[layer], lambda_k2[layer],
                           subln[layer], w_o_attn[layer], w_out[layer])
        x = x + g1 * rmsnorm(mix, norm_post_mix[layer])
        h = rmsnorm(x, norm_pre_ffn[layer]) * (1.0 + sc2) + sh2
        f = (jax.nn.silu(h @ w_gate[layer]) * (h @ w_up[layer])) @ w_down[layer]
        x = x + g2 * rmsnorm(f, norm_post_ffn[layer])
    return x
```

```python
import math
import contextlib
import numpy as np
import concourse.bass as bass
import concourse.mybir as mybir
from concourse.bass_utils import run_bass_kernel_spmd

F32 = mybir.dt.float32
BF16 = mybir.dt.bfloat16
I32 = mybir.dt.int32
ALU = mybir.AluOpType
AF = mybir.ActivationFunctionType

ENGS = ("pe", "act", "dve", "pool", "sp")
EPS = 1e-6
NEG = -30000.0
LAMBDA_INIT = 0.8 - 0.6 * math.exp(0.0)


class Res:
    __slots__ = ("name", "w", "r", "excl")

    def __init__(self, name="", excl=False):
        self.name = name
        self.w = None
        self.r = []
        self.excl = excl


class Op:
    __slots__ = ("eng", "fn", "deps", "sig", "cnt", "dma", "dslot", "dval", "bar")

    def __init__(self, eng, fn, dma):
        self.eng = eng
        self.fn = fn
        self.deps = []
        self.sig = False
        self.cnt = 0
        self.dma = dma
        self.dslot = None
        self.dval = 0
        self.bar = False


class Prog:
    KD = 8

    def __init__(self, nc):
        self.nc = nc
        self.ops = []
        self.per = {e: [] for e in ENGS}

    def _add(self, eng, fn, reads, writes, dma):
        o = Op(eng, fn, dma)
        deps = set()
        xr = [r for r in reads if r.excl]
        if xr:
            reads = [r for r in reads if not r.excl]
            writes = list(writes) + [r for r in xr if r not in writes]
        for r in reads:
            if r.w is not None:
                deps.add(r.w)
        for r in writes:
            if r.w is not None:
                deps.add(r.w)
            for x in r.r:
                deps.add(x)
        for r in reads:
            r.r.append(o)
        for r in writes:
            r.w = o
            r.r = []
        deps.discard(o)
        o.deps = list(deps)
        self.ops.append(o)
        self.per[eng].append(o)
        return o

    def op(self, eng, fn, reads=(), writes=()):
        return self._add(eng, fn, reads, writes, False)

    def dma(self, eng, out, in_, reads=(), writes=()):
        return self._add(eng, lambda e: e.dma_start(out=out, in_=in_), reads, writes, True)

    def barrier(self):
        deps = []
        for e in ENGS:
            real = [o for o in self.per[e] if not o.bar and not o.dma]
            if real:
                deps.append(real[-1])
            deps += [o for o in self.per[e] if o.dma][-self.KD:]
        for e in ENGS:
            o = Op(e, None, False)
            o.bar = True
            o.deps = list(deps)
            self.ops.append(o)
            self.per[e].append(o)

    def emit(self, final_waits=()):
        nc = self.nc
        for o in self.ops:
            for d in o.deps:
                if d.dma:
                    continue
                if d.eng == o.eng and o.eng == "pe":
                    continue
                d.sig = True
        ndma = {e: 0 for e in ENGS}
        for e in ENGS:
            c = 0
            for o in self.per[e]:
                if o.dma:
                    i = ndma[e]
                    ndma[e] += 1
                    o.dslot = (e, i % self.KD)
                    o.dval = 16 * (i // self.KD + 1)
                elif o.sig:
                    c += 1
                    o.cnt = c
        with contextlib.ExitStack() as st:
            sem = {e: st.enter_context(nc.semaphore("s_" + e)) for e in ENGS}
            dsem = {}
            for e in ENGS:
                for k in range(min(self.KD, ndma[e])):
                    dsem[(e, k)] = st.enter_context(nc.semaphore("d_%s%d" % (e, k)))
            block = st.enter_context(nc.Block())
            engobj = {"pe": block.tensor, "act": block.scalar, "dve": block.vector,
                      "pool": block.gpsimd, "sp": block.sync}

            def body(e):
                def f(eng):
                    seen = {}
                    for o in self.per[e]:
                        for d in o.deps:
                            if d.dma:
                                key = ("d",) + d.dslot
                                if seen.get(key, 0) < d.dval:
                                    eng.wait_ge(dsem[d.dslot], d.dval)
                                    seen[key] = d.dval
                            else:
                                if d.eng == e and e == "pe":
                                    continue
                                if seen.get(d.eng, 0) < d.cnt:
                                    eng.wait_ge(sem[d.eng], d.cnt)
                                    seen[d.eng] = d.cnt
                        if o.bar:
                            continue
                        if o.dma and o.dval > 16:
                            key = ("d",) + o.dslot
                            if seen.get(key, 0) < o.dval - 16:
                                eng.wait_ge(dsem[o.dslot], o.dval - 16)
                                seen[key] = o.dval - 16
                        ins = o.fn(eng)
                        if o.dma:
                            ins.then_inc(dsem[o.dslot], 16)
                        elif o.sig:
                            ins.then_inc(sem[e], 1)
                    if e == "sp":
                        for o in final_waits:
                            eng.wait_ge(dsem[o.dslot], o.dval)
                return f

            for e in ENGS:
                engobj[e](body(e))


class Ring:
    U = 0

    def __init__(self, nc, st, name, shape, dt, n):
        Ring.U += 1
        self.t = [st.enter_context(nc.sbuf_tensor("r%d_%s%d" % (Ring.U, name, i), shape, dt)) for i in range(n)]
        self.r = [Res(name) for _ in range(n)]
        self.i = 0

    def get(self):
        k = self.i % len(self.t)
        self.i += 1
        return self.t[k], self.r[k]


class _Stop(Exception):
    pass


def build(stop=None, debug=False):
    nc = bass.Bass("TRN2", target_bir_lowering=False)

    def din(name, shape, dt=F32):
        return nc.dram_tensor(name, shape, dt, kind="ExternalInput").ap()

    def dsc(name, shape, dt):
        return nc.dram_tensor(name, shape, dt, kind="ExternalOutput" if debug else "Internal").ap()

    xin = din("xin", [4096, 1024])
    pos_d = din("pos", [128, 4096], I32)
    invf_d = din("invf", [128, 1])
    valid_d = din("valid", [128, 1])
    ccol_d = din("ccol", [128, 8])
    w_ada = din("w_ada", [1024, 6144])
    bada_col_d = din("bada_col", [128, 48])
    bada_g_d = din("bada_g", [128, 2, 1024])
    npre_mix_d = din("npre_mix", [128, 8])
    npre_ffn_d = din("npre_ffn", [128, 8])
    npost_mix_d = din("npost_mix", [128, 1024])
    npost_ffn_d = din("npost_ffn", [128, 1024])
    w_in = din("w_in", [1024, 10272])
    convw_d = din("convw", [128, 24, 4])
    convb_d = din("convb", [128, 24])
    dtb_d = din("dtb", [128, 32])
    alog_d = din("alog", [128, 32])
    dskip_d = din("dskip", [128, 32])
    ssdn_d = din("ssdn", [128, 16])
    w_o_ssd = din("w_o_ssd", [2048, 1024])
    lam_d = din("lam", [128, 4, 64])
    subln_d = din("subln", [128, 1])
    w_o_attn = din("w_o_attn", [1024, 1024])
    w_out = din("w_out", [1024, 1024])
    w_gate = din("w_gate", [1024, 2816])
    w_up = din("w_up", [1024, 2816])
    w_down = din("w_down", [2816, 1024])
    ident_d = din("ident", [128, 128])
    prot_d = din("prot", [128, 128])
    negm_d = din("negm", [128, 128])
    utri_d = din("utri", [128, 128])
    sel_d = din("sel", [8, 8, 128])
    out = nc.dram_tensor("out", [2048, 1024], F32, kind="ExternalOutput").ap()

    kpre_d = dsc("kpre", [8, 128, 2048], BF16)
    vpre_d = dsc("vpre", [8, 128, 16 * 130], BF16)
    ynT_d = dsc("ynT", [16, 128, 2048], BF16)
    gT_d = dsc("gT", [16, 128, 2048], BF16)
    x1_d = dsc("x1", [2048, 1024], F32)
    oT_d = dsc("oTd", [8, 128, 2048], BF16)

    P = Prog(nc)
    dbg_ops = []

    def ck(name, **dumps):
        if stop != name:
            return
        P.barrier()
        for k, ap in dumps.items():
            d = nc.dram_tensor("dbg_" + k, list(ap.shape), ap.dtype, kind="ExternalOutput").ap()
            dbg_ops.append(P.dma("sp", d, ap))
        raise _Stop()

    try:
        _build_body(nc, P, ck, locals())
    except _Stop:
        P.emit(final_waits=dbg_ops)
    return nc


def _build_body(nc, P, ck, L):
    (xin, pos_d, invf_d, valid_d, ccol_d, w_ada, bada_col_d, bada_g_d, npre_mix_d, npre_ffn_d, npost_mix_d, npost_ffn_d,
     w_in, convw_d, convb_d, dtb_d, alog_d, dskip_d, ssdn_d, w_o_ssd, lam_d, subln_d, w_o_attn, w_out, w_gate, w_up, w_down,
     ident_d, prot_d, negm_d, utri_d, sel_d, out, kpre_d, vpre_d, ynT_d, gT_d, x1_d, oT_d) = [L[k] for k in (
        "xin pos_d invf_d valid_d ccol_d w_ada bada_col_d bada_g_d npre_mix_d npre_ffn_d npost_mix_d npost_ffn_d "
        "w_in convw_d convb_d dtb_d alog_d dskip_d ssdn_d w_o_ssd lam_d subln_d w_o_attn w_out w_gate w_up w_down "
        "ident_d prot_d negm_d utri_d sel_d out kpre_d vpre_d ynT_d gT_d x1_d oT_d").split()]
    with contextlib.ExitStack() as G:
        uniq = [0]

        def sb(name, shape, dt, st=G):
            uniq[0] += 1
            return st.enter_context(nc.sbuf_tensor("s%d_%s" % (uniq[0], name), shape, dt))

        banks = [G.enter_context(nc.psum_tensor("bk%d" % i, [128, 512], F32)) for i in range(8)]
        bres = [Res("bk%d" % i, excl=True) for i in range(8)]
        bank_i = [0]

        def bank(lo=0, hi=8):
            k = lo + bank_i[0] % (hi - lo)
            bank_i[0] += 1
            return banks[k], bres[k]

        def mm(o, lhsT, rhs, start, stop, reads, writes, sgc=False):
            if sgc:
                P.op("pe", lambda e: e.matmul(o, lhsT=lhsT, rhs=rhs, start=start, stop=stop, skip_group_check=True), reads, writes)
            else:
                P.op("pe", lambda e: e.matmul(o, lhsT=lhsT, rhs=rhs, start=start, stop=stop), reads, writes)

        def tr(o, i, ident, reads, writes):
            P.op("pe", lambda e: e.transpose(out=o, in_=i, identity=ident), reads, writes)

        def act(o, i, func, reads, writes, bias=None, scale=None, accum=None):
            kw = {}
            if bias is not None:
                kw["bias"] = bias
            if scale is not None:
                kw["scale"] = scale
            if accum is not None:
                kw["accum_out"] = accum
            P.op("act", lambda e: e.activation(out=o, in_=i, func=func, **kw), reads, writes)

        def tsc(eng, o, i, s1, s2, op0, op1, reads, writes):
            if s2 is None:
                P.op(eng, lambda e: e.tensor_scalar(out=o, in0=i, scalar1=s1, scalar2=None, op0=op0), reads, writes)
            else:
                P.op(eng, lambda e: e.tensor_scalar(out=o, in0=i, scalar1=s1, scalar2=s2, op0=op0, op1=op1), reads, writes)

        def tt(eng, o, a, b, op, reads, writes):
            P.op(eng, lambda e: e.tensor_tensor(out=o, in0=a, in1=b, op=op), reads, writes)

        def stt(eng, o, a, s, b, op0, op1, reads, writes):
            P.op(eng, lambda e: e.scalar_tensor_tensor(out=o, in0=a, scalar=s, in1=b, op0=op0, op1=op1), reads, writes)

        def cp(eng, o, i, reads, writes):
            P.op(eng, lambda e: e.tensor_copy(out=o, in_=i), reads, writes)

        def ms(eng, o, v, writes):
            P.op(eng, lambda e: e.memset(o, v), (), writes)

        def recip(o, i, reads, writes):
            P.op("dve", lambda e: e.reciprocal(out=o, in_=i), reads, writes)

        def load(name, src, shape, dt=F32, st=G):
            t = sb(name, shape, dt, st)
            r = Res(name)
            P.dma("sp", t[:], src, writes=[r])
            return t, r

        identf, r_identf = load("identf", ident_d, [128, 128])
        protf, r_protf = load("protf", prot_d, [128, 128])
        negmf, r_negmf = load("negmf", negm_d, [128, 128])
        utri, r_utri = load("utri", utri_d, [128, 128])
        valid, r_valid = load("valid", valid_d, [128, 1])
        invf, r_invf = load("invf", invf_d, [128, 1])
        ccol, r_ccol = load("ccol", ccol_d, [128, 8])
        bada_col, r_bada_col = load("bada_col", bada_col_d, [128, 48])
        npre_mix, r_npre_mix = load("npre_mix", npre_mix_d, [128, 8])
        npre_ffn, r_npre_ffn = load("npre_ffn", npre_ffn_d, [128, 8])
        convw, r_convw = load("convw", convw_d, [128, 24, 4])
        convb, r_convb = load("convb", convb_d, [128, 24])
        dtb, r_dtb = load("dtb", dtb_d, [128, 32])
        alog, r_alog = load("alog", alog_d, [128, 32])
        dskip, r_dskip = load("dskip", dskip_d, [128, 32])
        ssdn, r_ssdn = load("ssdn", ssdn_d, [128, 16])
        lamt, r_lamt = load("lamt", lam_d, [128, 4, 64])
        subln, r_subln = load("subln", subln_d, [128, 1])

        identb = sb("identb", [128, 128], BF16)
        r_identb = Res()
        cp("pool", identb[:], identf[:], [r_identf], [r_identb])
        protb = sb("protb", [128, 128], BF16)
        r_protb = Res()
        cp("pool", protb[:], protf[:], [r_protf], [r_protb])
        negmb = sb("negmb", [128, 128], BF16)
        r_negmb = Res()
        cp("pool", negmb[:], negmf[:], [r_negmf], [r_negmb])
        onesf = sb("onesf", [128, 128], F32)
        r_onesf = Res()
        ms("pool", onesf[:], 1.0, [r_onesf])
        Arep = sb("Arep", [128, 32], F32)
        r_Arep = Res()
        act(Arep[:], alog[:], AF.Exp, [r_alog], [r_Arep])
        tsc("dve", Arep[:], Arep[:], -1.0, None, ALU.mult, None, [r_Arep], [r_Arep])
        lprod = sb("lprod", [128, 2, 64], F32)
        lsum = sb("lsum", [128, 2], F32)
        neglam = sb("neglam", [128, 1], F32)
        r_lam = Res()
        r_neglam = Res()
        tt("dve", lprod[:, 0, :], lamt[:, 0, :], lamt[:, 1, :], ALU.mult, [r_lamt], [r_lam])
        tt("dve", lprod[:, 1, :], lamt[:, 2, :], lamt[:, 3, :], ALU.mult, [r_lamt, r_lam], [r_lam])
        P.op("dve", lambda e: e.reduce_sum(out=lsum[:], in_=lprod[:], axis=mybir.AxisListType.X), [r_lam], [r_lam])
        act(lsum[:], lsum[:], AF.Exp, [r_lam], [r_lam])
        tt("dve", neglam[:], lsum[:, 1:2], lsum[:, 0:1], ALU.subtract, [r_lam], [r_neglam])
        tsc("dve", neglam[:], neglam[:], -LAMBDA_INIT, None, ALU.add, None, [r_neglam], [r_neglam])
        sublnw = sb("sublnw", [128, 1], F32)
        r_sublnw = Res()
        tsc("dve", sublnw[:], subln[:], 1.0 - LAMBDA_INIT, None, ALU.mult, None, [r_subln], [r_sublnw])

        state0 = sb("state0", [128, 4, 512], F32)
        r_state0 = [Res() for _ in range(4)]
        utail = sb("utail", [128, 24, 3], F32)
        r_utail = [Res() for _ in range(24)]
        modcol = sb("modcol", [128, 48], F32)
        r_modcol = Res()
        g1n = sb("g1n", [128, 1024], F32)
        g2n = sb("g2n", [128, 1024], F32)
        r_g1n = Res()
        r_g2n = Res()
        gsc1 = sb("gsc1", [128, 8], F32)
        gsc2 = sb("gsc2", [128, 8], F32)
        r_gsc = Res()

        with contextlib.ExitStack() as S0:
            cond = sb("cond", [128, 8], F32, S0)
            r_cond = Res()
            act(cond[:], ccol[:], AF.Silu, [r_ccol], [r_cond])
            condb = sb("condb", [128, 8, 128], F32, S0)
            r_condb = Res()
            cp("dve", condb[:], cond[:].unsqueeze(2).to_broadcast([128, 8, 128]), [r_cond], [r_condb])
            bada_g, r_bada_g = load("bada_g", bada_g_d, [128, 2, 1024], F32, S0)
            npost_mix, r_npost_mix = load("npost_mix", npost_mix_d, [128, 1024], F32, S0)
            npost_ffn, r_npost_ffn = load("npost_ffn", npost_ffn_d, [128, 1024], F32, S0)
            waR = Ring(nc, S0, "wa", [128, 8, 512], F32, 2)
            pcol, r_pcol = banks[0], bres[0]
            for q in range(12):
                wa, r_wa = waR.get()
                P.dma("sp", wa[:], w_ada[:, q * 512:(q + 1) * 512].rearrange("(c p) n -> p c n", p=128), writes=[r_wa])
                if q in (4, 5, 10, 11):
                    gi = 0 if q < 6 else 1
                    half = q % 2
                    bk, rbk = bank(1, 8)
                    for kc in range(8):
                        mm(bk[:, :], condb[:, kc, :], wa[:, kc, :], kc == 0, kc == 7, [r_condb, r_wa], [rbk])
                    dst = (g1n if gi == 0 else g2n)
                    rd = (r_g1n if gi == 0 else r_g2n)
                    tt("dve", dst[:, half * 512:(half + 1) * 512], bk[:, :], bada_g[:, gi, half * 512:(half + 1) * 512],
                       ALU.add, [rbk, r_bada_g], [rd])
                else:
                    for jj in range(4):
                        j = q * 4 + jj
                        for kc in range(8):
                            mm(pcol[:, j:j + 1], wa[:, kc, jj * 128:(jj + 1) * 128], cond[:, kc:kc + 1], kc == 0, kc == 7,
                               [r_cond, r_wa], [r_pcol])
            ms("dve", modcol[:], 0.0, [r_modcol])
            tt("dve", modcol[:, 0:16], pcol[:, 0:16], bada_col[:, 0:16], ALU.add, [r_pcol, r_bada_col], [r_modcol])
            tt("dve", modcol[:, 24:40], pcol[:, 24:40], bada_col[:, 24:40], ALU.add, [r_pcol, r_bada_col], [r_modcol])
            tt("dve", g1n[:], g1n[:], npost_mix[:], ALU.mult, [r_g1n, r_npost_mix], [r_g1n])
            tt("dve", g2n[:], g2n[:], npost_ffn[:], ALU.mult, [r_g2n, r_npost_ffn], [r_g2n])
            stt("dve", gsc1[:], modcol[:, 8:16], 1.0, npre_mix[:], ALU.add, ALU.mult, [r_modcol, r_npre_mix], [r_gsc])
            stt("dve", gsc2[:], modcol[:, 32:40], 1.0, npre_ffn[:], ALU.add, ALU.mult, [r_modcol, r_npre_ffn, r_gsc], [r_gsc])
            P.barrier()
            ck("p0", modcol=modcol[:], g1n=g1n[:], g2n=g2n[:], gsc1=gsc1[:], gsc2=gsc2[:], neglam=neglam[:])
        sh1 = modcol[:, 0:8]
        sh2 = modcol[:, 24:32]

        def norm_transpose(xt, r_xt, gsc, sh, dstT, tok0, r_dst, sqj, r_sqj, stat, r_stat, xnR, defer=None):
            act(sqj[:], xt[:], AF.Square, [r_xt], [r_sqj, r_stat], accum=stat[:, 0:1])
            act(stat[:, 1:2], stat[:, 0:1], AF.Ln, [r_stat], [r_stat], bias=EPS, scale=1.0 / 1024)
            act(stat[:, 2:3], stat[:, 1:2], AF.Exp, [r_stat], [r_stat], scale=-0.5)
            xn, r_xn = xnR.get()
            tsc("dve", xn[:], xt[:], stat[:, 2:3], None, ALU.mult, None, [r_xt, r_stat], [r_xn])

            def part2():
                bk, rbk = bank()
                bv = bk[:].bitcast(BF16)
                for c in range(8):
                    tr(bv[:, c * 128:(c + 1) * 128], xn[:, c * 128:(c + 1) * 128], identb[:], [r_xn, r_identb], [rbk])
                for c in range(8):
                    o = dstT[:, c, tok0:tok0 + 128]
                    i = bv[:, c * 128:(c + 1) * 128]
                    if c % 2 == 0:
                        tsc("dve", o, i, gsc[:, c:c + 1], sh[:, c:c + 1], ALU.mult, ALU.add, [rbk, r_gsc, r_modcol], [r_dst])
                    else:
                        act(o, i, AF.Identity, [rbk, r_gsc, r_modcol], [r_dst], bias=sh[:, c:c + 1], scale=gsc[:, c:c + 1])

            if defer is None:
                part2()
            else:
                defer.append(part2)

        for pas in range(2):
            own = pas == 1
            T0 = pas * 2048
            with contextlib.ExitStack() as SP:
                hT = sb("hT", [128, 8, 2048], BF16, SP)
                r_hT = [Res() for _ in range(16)]

                def hres(tok0, ntok):
                    return r_hT[tok0 // 128:(tok0 + ntok + 127) // 128]

                wst = Ring(nc, SP, "wst", [128, 8, 128], F32, 2)
                wbf = Ring(nc, SP, "wbf", [128, 8, 128], BF16, 3)

                def load_wblk(col0, ncols):
                    s_t, s_r = wst.get()
                    b_t, b_r = wbf.get()
                    P.dma("sp", s_t[:, :, 0:ncols], w_in[:, col0:col0 + ncols].rearrange("(c p) n -> p c n", p=128),
                          writes=[s_r])
                    act(b_t[:, :, 0:ncols], s_t[:, :, 0:ncols], AF.Copy, [s_r], [b_r])
                    return b_t, b_r

                def proj_fm(wb, r_wb, tok0, ntok, bk, rbk):
                    for kc in range(8):
                        mm(bk[:, 0:ntok], wb[:, kc, :], hT[:, kc, tok0:tok0 + ntok], kc == 0, kc == 7,
                           [r_wb] + hres(tok0, ntok), [rbk])

                with contextlib.ExitStack() as SA:
                    xR = Ring(nc, SA, "xt", [128, 1024], F32, 3)
                    xnR = Ring(nc, SA, "xn", [128, 1024], BF16, 2)
                    sqj = sb("sqj", [128, 1024], BF16, SA)
                    r_sqj = Res()
                    statR = Ring(nc, SA, "stat", [128, 4], F32, 2)
                    prev_p = []
                    for t in range(16):
                        xt, r_xt = xR.get()
                        P.dma("sp", xt[:], xin[T0 + t * 128:T0 + (t + 1) * 128, :], writes=[r_xt])
                        stat, r_stat = statR.get()
                        cur_p = []
                        norm_transpose(xt, r_xt, gsc1, sh1, hT, t * 128, r_hT[t], sqj, r_sqj, stat, r_stat, xnR, defer=cur_p)
                        while prev_p:
                            prev_p.pop(0)()
                        prev_p = cur_p
                    while prev_p:
                        prev_p.pop(0)()
                    P.barrier()
                    ck("A%d" % pas, hT=hT[:])

                def run_ssd(SS, inject):
                    xs_tm = sb("xs_tm", [128, 16, 512], BF16, SS)
                    r_xs = [Res() for _ in range(16)]
                    Btm = sb("Btm", [128, 16, 128], BF16, SS)
                    r_Btm = [Res() for _ in range(16)]
                    uR = Ring(nc, SS, "u", [128, 3 + 1024], F32, 2)
                    accR = Ring(nc, SS, "acc", [128, 1024], F32, 1)
                    xcR = Ring(nc, SS, "xc", [128, 1024], BF16, 2)
                    dtg = sb("dtg", [128, 16, 8], F32, SS)
                    ag = sb("ag", [128, 16, 8], F32, SS)
                    acsg = sb("acsg", [128, 16, 8], F32, SS)
                    dteg = sb("dteg", [128, 16, 8], F32, SS)
                    ecsg = sb("ecsg", [128, 16, 8], F32, SS)
                    cdg = sb("cdg", [128, 16, 8], F32, SS)
                    wdteg = sb("wdteg", [128, 16, 8], F32, SS)
                    tmp8 = sb("tmp8", [128, 16, 8], F32, SS)
                    r_dec = Res()
                    prev = sb("prev", [128, 512], F32, SS)
                    r_prev = Res()
                    ptmpR = Ring(nc, SS, "ptmp", [128, 512], F32, 1)
                    xdteR = Ring(nc, SS, "xdte", [128, 512], BF16, 4)
                    if own:
                        BT = sb("BT", [128, 2048], BF16, SS)
                        CT = sb("CT", [128, 2048], BF16, SS)
                        r_BT = [Res() for _ in range(2)]
                        r_CT = [Res() for _ in range(2)]
                        zs = sb("zs", [128, 16, 512], BF16, SS)
                        r_zs = [Res() for _ in range(16)]
                        ynT = sb("ynT", [128, 4, 2048], BF16, SS)
                        r_ynT = Res()
                        sel, r_sel = load("sel", sel_d, [8, 8, 128], F32, SS)
                        xdtR = Ring(nc, SS, "xdt", [128, 512], BF16, 3)
                        pbfR = Ring(nc, SS, "pbf", [128, 512], BF16, 2)
                        scTR = Ring(nc, SS, "scT", [128, 128], F32, 4)
                        acsTR = Ring(nc, SS, "acsT", [8, 128], F32, 4)
                        ahR = Ring(nc, SS, "ah", [8, 2, 128], BF16, 4)
                        bdR = Ring(nc, SS, "bd", [8, 2, 512], BF16, 3)
                        negm4b = sb("negm4b", [128, 512], BF16, SS)
                        r_negm4b = Res()
                        cp("pool", negm4b[:].rearrange("p (h l) -> p h l", h=4), negmb[:].unsqueeze(1).to_broadcast([128, 4, 128]), [r_negmb], [r_negm4b])
                        selb = sb("selb", [8, 8, 128], BF16, SS)
                        nselb = sb("nselb", [8, 8, 128], BF16, SS)
                        onesb = sb("onesb", [8, 128], BF16, SS)
                        r_selb = Res()
                        r_nselb = Res()
                        r_onesb = Res()
                        cp("pool", selb[:], sel[:], [r_sel], [r_selb])
                        tsc("dve", nselb[:], sel[:], -1.0, None, ALU.mult, None, [r_sel], [r_nselb])
                        ms("pool", onesb[:], 1.0, [r_onesb])
                        decR = Ring(nc, SS, "dec", [128, 512], F32, 2)
                        MTR = Ring(nc, SS, "MT", [128, 8, 128], BF16, 3)
                        y1R = Ring(nc, SS, "y1", [128, 512], F32, 1)
                        y2R = Ring(nc, SS, "y2", [128, 512], F32, 1)
                        y3R = Ring(nc, SS, "y3", [128, 512], F32, 2)
                        ynR = Ring(nc, SS, "yn", [128, 512], BF16, 2)
                        sqy = sb("sqy", [128, 512], BF16, SS)
                        r_sqy = Res()
                        ystR = Ring(nc, SS, "yst", [128, 4], F32, 2)

                    pending_tail = []

                    def flush_tail():
                        while pending_tail:
                            pending_tail.pop(0)()

                    def conv_block(blk, col0, kind, first_tok=0):
                        wb, r_wb = load_wblk(col0, 128)
                        u_prev = None
                        for q in range(first_tok // 1024, 2):
                            u, r_u = uR.get()
                            if q == first_tok // 1024:
                                if own:
                                    cp("pool", u[:, 0:3], utail[:, blk, :], [r_utail[blk]], [r_u])
                                else:
                                    ms("pool", u[:, 0:3], 0.0, [r_u])
                            else:
                                cp("pool", u[:, 0:3], u_prev[0][:, 1024:1027], [u_prev[1]], [r_u])
                            g0 = 0
                            if first_tok > q * 1024:
                                g0 = (first_tok - q * 1024) // 512
                                ms("pool", u[:, 3:3 + g0 * 512], 0.0, [r_u])
                            for gq in range(g0, 2):
                                bk, rbk = bank()
                                proj_fm(wb, r_wb, q * 1024 + gq * 512, 512, bk, rbk)
                                if gq == 1:
                                    flush_tail()
                                if own:
                                    act(u[:, 3 + gq * 512:3 + (gq + 1) * 512], bk[:, :], AF.Copy, [rbk], [r_u])
                                else:
                                    act(u[:, 3 + gq * 512:3 + (gq + 1) * 512], bk[:, :], AF.Copy, [rbk, r_valid], [r_u],
                                        scale=valid[:, 0:1])
                            u_prev = (u, r_u)
                            if (not own) and q == 1:
                                cp("pool", utail[:, blk, :], u[:, 1024:1027], [r_u], [r_utail[blk]])
                            if kind == "C" and not own:
                                continue
                            acc, r_acc = accR.get()
                            tsc("dve", acc[:], u[:, 3:1027], convw[:, blk, 3:4], convb[:, blk:blk + 1], ALU.mult, ALU.add,
                                [r_u, r_convw, r_convb], [r_acc])
                            for k in (2, 1, 0):
                                stt("dve", acc[:], u[:, k:k + 1024], convw[:, blk, k:k + 1], acc[:], ALU.mult, ALU.add,
                                    [r_u, r_convw, r_acc], [r_acc])
                            xc, r_xc = xcR.get()
                            act(xc[:], acc[:], AF.Silu, [r_acc], [r_xc])
                            if inject is not None:
                                inject()
                            if kind == "C":
                                cp("pool", CT[:, q * 1024:(q + 1) * 1024], xc[:], [r_xc], [r_CT[q]])
                                continue
                            if kind == "B" and own:
                                cp("pool", BT[:, q * 1024:(q + 1) * 1024], xc[:], [r_xc], [r_BT[q]])
                            def tail(xc=xc, r_xc=r_xc, q=q, kind=kind):
                                bk, rbk = bank()
                                bv = bk[:].bitcast(BF16)
                                for i in range(8):
                                    tr(bv[:, i * 128:(i + 1) * 128], xc[:, i * 128:(i + 1) * 128], identb[:], [r_xc, r_identb], [rbk])
                                if kind == "B":
                                    cp("dve", Btm[:, q * 8:(q + 1) * 8, :], bv.rearrange("p (t c) -> p t c", t=8), [rbk],
                                       r_Btm[q * 8:(q + 1) * 8])
                                else:
                                    j = kind
                                    act(xs_tm[:, q * 8:(q + 1) * 8, j * 128:(j + 1) * 128], bv.rearrange("p (t c) -> p t c", t=8),
                                        AF.Copy, [rbk], r_xs[q * 8:(q + 1) * 8])
                            pending_tail.append(tail)

                    for g in range(4):
                        wdt, r_wdt = load_wblk(5120 + 8 * g, 8)
                        bk, rbk = bank()
                        for c in range(16):
                            for kc in range(8):
                                mm(bk[:, c * 8:(c + 1) * 8], hT[:, kc, c * 128:(c + 1) * 128], wdt[:, kc, 0:8], kc == 0, kc == 7,
                                   [r_wdt, r_hT[c]], [rbk])
                        bkv = bk[:, 0:128].rearrange("p (c h) -> p c h", h=8)
                        tt("dve", tmp8[:], bkv, dtb[:, 8 * g:8 * g + 8].unsqueeze(1).to_broadcast([128, 16, 8]), ALU.add,
                           [rbk, r_dtb], [r_dec])
                        act(tmp8[:], tmp8[:], AF.Exp, [r_dec], [r_dec])
                        act(dtg[:], tmp8[:], AF.Ln, [r_dec], [r_dec], bias=1.0)
                        tt("dve", ag[:], dtg[:], Arep[:, 8 * g:8 * g + 8].unsqueeze(1).to_broadcast([128, 16, 8]), ALU.mult,
                           [r_dec, r_Arep], [r_dec])
                        bkc, rbkc = bank()
                        bkt, rbkt = bank()
                        for c in range(16):
                            mm(bkc[:, c * 8:(c + 1) * 8], utri[:], ag[:, c, :], True, True, [r_utri, r_dec], [rbkc])
                            mm(bkt[:, c * 8:(c + 1) * 8], onesf[:], ag[:, c, :], True, True, [r_onesf, r_dec], [rbkt])
                        bkcv = bkc[:, 0:128].rearrange("p (c h) -> p c h", h=8)
                        bktv = bkt[:, 0:128].rearrange("p (c h) -> p c h", h=8)
                        cp("dve", acsg[:], bkcv, [rbkc], [r_dec])
                        act(ecsg[:], acsg[:], AF.Exp, [r_dec], [r_dec])
                        act(cdg[:], bktv, AF.Exp, [rbkt, r_dec], [r_dec])
                        tt("dve", tmp8[:], bktv, acsg[:], ALU.subtract, [rbkt, r_dec], [r_dec])
                        act(dteg[:], tmp8[:], AF.Exp, [r_dec], [r_dec])
                        tt("dve", wdteg[:], dteg[:], dtg[:], ALU.mult, [r_dec], [r_dec])

                        for j in range(4):
                            conv_block(4 * g + j, 2048 + (4 * g + j) * 128, j)
                        conv_block(16 + g, 2048 + 2048 + g * 128, "B")
                        if own:
                            conv_block(20 + g, 2048 + 2560 + g * 128, "C")
                        else:
                            conv_block(20 + g, 2048 + 2560 + g * 128, "C", first_tok=1536)
                        flush_tail()
                        if own:
                            for j in range(4):
                                wz, r_wz = load_wblk((4 * g + j) * 128, 128)
                                for t in range(16):
                                    if t % 4 == 0:
                                        bk, rbk = bank()
                                    for kc in range(8):
                                        mm(bk[:, (t % 4) * 128:(t % 4 + 1) * 128], hT[:, kc, t * 128:(t + 1) * 128], wz[:, kc, :],
                                           kc == 0, kc == 7, [r_wz, r_hT[t]], [rbk])
                                    if t % 4 == 3:
                                        act(zs[:, t - 3:t + 1, j * 128:(j + 1) * 128], bk[:, :].rearrange("p (t c) -> p t c", t=4),
                                            AF.Silu, [rbk], r_zs[t - 3:t + 1])

                        if own:
                            cp("dve", prev[:], state0[:, g, :], [r_state0[g]], [r_prev])
                        else:
                            ms("dve", prev[:], 0.0, [r_prev])
                        st1 = {}

                        def xsv_(c):
                            return xs_tm[:, c, :].rearrange("p (h d) -> p h d", h=8)

                        def S1(c):
                            d = {}
                            xdte, r_xdte = xdteR.get()
                            tt("pool", xdte[:].rearrange("p (h d) -> p h d", h=8), xsv_(c),
                               wdteg[:, c, :].unsqueeze(2).to_broadcast([128, 8, 64]), ALU.mult, [r_xs[c], r_dec], [r_xdte])
                            d["xdte"] = (xdte, r_xdte)
                            if own:
                                xdt, r_xdt = xdtR.get()
                                tt("pool", xdt[:].rearrange("p (h d) -> p h d", h=8), xsv_(c),
                                   dtg[:, c, :].unsqueeze(2).to_broadcast([128, 8, 64]), ALU.mult, [r_xs[c], r_dec], [r_xdt])
                                d["xdt"] = (xdt, r_xdt)
                                cs = slice(c * 128, (c + 1) * 128)
                                rB = r_BT[c // 8]
                                rC = r_CT[c // 8]
                                bks, rbks = bank()
                                mm(bks[:, 0:128], BT[:, cs], CT[:, cs], True, True, [rB, rC], [rbks])
                                scT, r_scT = scTR.get()
                                act(scT[:], bks[:, 0:128], AF.Copy, [rbks], [r_scT])
                                bka, rbka = bank()
                                mm(bka[0:8, 0:128], ag[:, c, :], utri[:], True, True, [r_dec, r_utri], [rbka])
                                acsT, r_acsT = acsTR.get()
                                ah, r_ah = ahR.get()
                                act(acsT[:], bka[0:8, 0:128], AF.Copy, [rbka], [r_acsT])
                                act(ah[:, 0, :], bka[0:8, 0:128], AF.Copy, [rbka], [r_ah])
                                tt("pool", ah[:, 1, :], acsT[:], ah[:, 0, :], ALU.subtract, [r_acsT, r_ah], [r_ah])
                                d["ah"] = (ah, r_ah)
                                d["scT"] = (scT, r_scT)
                            st1[c] = d

                        def S1r(c):
                            d = st1[c]
                            if True:
                                ah, r_ah = d["ah"]
                                MT, r_MT = MTR.get()
                                decs = []
                                for hh in range(2):
                                    bkr, rbkr = bank()
                                    nsv = nselb[:, hh * 4:(hh + 1) * 4, :].rearrange("k h l -> k (h l)")
                                    first = True
                                    for h4 in range(4):
                                        for t2_ in range(2):
                                            mm(bkr[:, h4 * 128:(h4 + 1) * 128], selb[:, hh * 4 + h4, :], ah[:, t2_, :], first, False,
                                               [r_selb, r_ah], [rbkr], sgc=True)
                                            first = False
                                    mm(bkr[:, :], ah[:, 0, :], nsv, False, False, [r_nselb, r_ah], [rbkr], sgc=True)
                                    mm(bkr[:, :], ah[:, 1, :], nsv, False, False, [r_nselb, r_ah], [rbkr], sgc=True)
                                    mm(bkr[:, :], identb[:], negm4b[:], False, True, [r_identb, r_negm4b], [rbkr], sgc=True)
                                    dec, r_dc = decR.get()
                                    act(dec[:], bkr[:, :], AF.Exp, [rbkr], [r_dc])
                                    decs.append((dec, r_dc))
                                d["MT"] = (MT, r_MT)
                                d["decs"] = decs

                        def S1b(c):
                            d = st1[c]
                            MT, r_MT = d["MT"]
                            scT, r_scT = d["scT"]
                            for hh in range(2):
                                dec, r_dc = d["decs"][hh]
                                tt("dve", MT[:, hh * 4:(hh + 1) * 4, :], dec[:].rearrange("p (h l) -> p h l", h=4),
                                   scT[:].unsqueeze(1).to_broadcast([128, 4, 128]), ALU.mult, [r_dc, r_scT], [r_MT])

                        def S2a(c, pbf, r_pbf):
                            d = st1[c]
                            MT, r_MT = d["MT"]
                            xdt, r_xdt = d["xdt"]
                            cs = slice(c * 128, (c + 1) * 128)
                            rC = r_CT[c // 8]
                            bky, rbky = bank()
                            for h in range(8):
                                mm(bky[:, h * 64:(h + 1) * 64], MT[:, h, :], xdt[:, h * 64:(h + 1) * 64], True, True,
                                   [r_MT, r_xdt], [rbky])
                            bko, rbko = bank()
                            mm(bko[:, :], CT[:, cs], pbf[:], True, True, [rC, r_pbf], [rbko])
                            y1, r_y1 = y1R.get()
                            tt("dve", y1[:].rearrange("p (h d) -> p h d", h=8), bko[:, :].rearrange("p (h d) -> p h d", h=8),
                               ecsg[:, c, :].unsqueeze(2).to_broadcast([128, 8, 64]), ALU.mult, [rbko, r_dec], [r_y1])
                            y2, r_y2 = y2R.get()
                            tt("dve", y2[:], bky[:, :], y1[:], ALU.add, [rbky, r_y1], [r_y2])
                            y3, r_y3 = y3R.get()
                            tt("pool", y3[:].rearrange("p (h d) -> p h d", h=8), xsv_(c),
                               dskip[:, 8 * g:8 * g + 8].unsqueeze(2).to_broadcast([128, 8, 64]), ALU.mult,
                               [r_xs[c], r_dskip], [r_y3])
                            tt("dve", y3[:], y3[:], y2[:], ALU.add, [r_y3, r_y2], [r_y3])
                            tt("dve", y3[:], y3[:], zs[:, c, :], ALU.mult, [r_y3, r_zs[c]], [r_y3])
                            yst, r_yst = ystR.get()
                            act(sqy[:], y3[:], AF.Square, [r_y3], [r_sqy, r_yst], accum=yst[:, 0:1])
                            act(yst[:, 1:2], yst[:, 0:1], AF.Ln, [r_yst], [r_yst], bias=EPS, scale=1.0 / 512)
                            act(yst[:, 2:3], yst[:, 1:2], AF.Exp, [r_yst], [r_yst], scale=-0.5)
                            yn, r_yn = ynR.get()
                            tsc("dve", yn[:], y3[:], yst[:, 2:3], None, ALU.mult, None, [r_y3, r_yst], [r_yn])
                            d["yn"] = (yn, r_yn)

                        def S2b(c):
                            yn, r_yn = st1[c]["yn"]
                            cs = slice(c * 128, (c + 1) * 128)
                            bkt2, rbkt2 = bank()
                            bv = bkt2[:].bitcast(BF16)
                            for j in range(4):
                                tr(bv[:, j * 128:(j + 1) * 128], yn[:, j * 128:(j + 1) * 128], identb[:], [r_yn, r_identb], [rbkt2])
                            act(ynT[:, :, cs], bv[:, 0:512].rearrange("p (j t) -> p j t", j=4), AF.Copy, [rbkt2], [r_ynT])
                            del st1[c]

                        def S3(c):
                            xdte, r_xdte = st1[c]["xdte"]
                            bkS, rbkS = bank()
                            mm(bkS[:, :], Btm[:, c, :], xdte[:], True, True, [r_Btm[c], r_xdte], [rbkS])
                            ptmp, r_ptmp = ptmpR.get()
                            tt("dve", ptmp[:].rearrange("p (h d) -> p h d", h=8), prev[:].rearrange("p (h d) -> p h d", h=8),
                               cdg[:, c, :].unsqueeze(2).to_broadcast([128, 8, 64]), ALU.mult, [r_prev, r_dec], [r_ptmp])
                            tt("dve", prev[:], bkS[:, :], ptmp[:], ALU.add, [rbkS, r_ptmp], [r_prev])

                        def mkpbf():
                            pbf, r_pbf = pbfR.get()
                            act(pbf[:], prev[:], AF.Copy, [r_prev], [r_pbf])
                            return pbf, r_pbf

                        if own:
                            for c0 in range(2):
                                S1(c0)
                                S1r(c0)
                                S1b(c0)
                            pb = mkpbf()
                            for c in range(16):
                                if c + 2 < 16:
                                    S1(c + 2)
                                pb_next = None
                                if c < 15:
                                    S3(c)
                                    pb_next = mkpbf()
                                S2a(c, *pb)
                                if c >= 1:
                                    S2b(c - 1)
                                if c + 2 < 16:
                                    S1r(c + 2)
                                    S1b(c + 2)
                                pb = pb_next
                            S2b(15)
                        else:
                            S1(0)
                            S1(1)
                            for c in range(16):
                                if c + 2 < 16:
                                    S1(c + 2)
                                S3(c)
                                del st1[c]
                                if inject is not None:
                                    inject()
                        if own:
                            for j in range(4):
                                P.dma("sp", ynT_d[g * 4 + j], ynT[:, j, :], reads=[r_ynT])
                        else:
                            tsc("dve", state0[:, g, :], prev[:], valid[:, 0:1], None, ALU.mult, None, [r_prev, r_valid],
                                [r_state0[g]])
                        ck("ssd%d_g%d" % (pas, g), state0=state0[:], xs_tm=xs_tm[:], Btm=Btm[:], dtg=dtg[:], dteg=dteg[:], cdg=cdg[:])
                    P.barrier()
                    ck("ssd%d" % pas, state0=state0[:], utail=utail[:])

                def run_att(SAt, ssd_call):
                    cosT = sb("cosT", [128, 2048], F32, SAt)
                    sinT = sb("sinT", [128, 2048], F32, SAt)
                    r_cos = Res()
                    r_sin = Res()
                    with contextlib.ExitStack() as SR:
                        pi_, r_pi = load("posi", pos_d[:, T0:T0 + 2048], [128, 2048], I32, SR)
                        ang = sb("ang", [128, 2048], F32, SR)
                        kf = sb("kf", [128, 2048], F32, SR)
                        ki = sb("ki", [128, 2048], I32, SR)
                        a2 = sb("a2", [128, 2048], F32, SR)
                        r_ang = Res()
                        r_kf = Res()
                        r_ki = Res()
                        r_a2 = Res()
                        cp("dve", ang[:], pi_[:], [r_pi], [r_ang])
                        tsc("dve", ang[:], ang[:], invf[:, 0:1], None, ALU.mult, None, [r_ang, r_invf], [r_ang])
                        for (dst, r_dst, shift) in ((sinT, r_sin, 0.0), (cosT, r_cos, math.pi / 2)):
                            tsc("dve", a2[:], ang[:], shift, None, ALU.add, None, [r_ang], [r_a2])
                            tsc("dve", kf[:], a2[:], 1.0 / (2 * math.pi), None, ALU.mult, None, [r_a2], [r_kf])
                            cp("dve", ki[:], kf[:], [r_kf], [r_ki])
                            cp("dve", kf[:], ki[:], [r_ki], [r_kf])
                            stt("dve", a2[:], kf[:], -2 * math.pi, a2[:], ALU.mult, ALU.add, [r_kf, r_a2], [r_a2])
                            tsc("dve", kf[:], a2[:], math.pi, -2 * math.pi, ALU.is_gt, ALU.mult, [r_a2], [r_kf])
                            tt("dve", a2[:], a2[:], kf[:], ALU.add, [r_a2, r_kf], [r_a2])
                            tsc("dve", kf[:], a2[:], -math.pi, 2 * math.pi, ALU.is_lt, ALU.mult, [r_a2], [r_kf])
                            tt("dve", a2[:], a2[:], kf[:], ALU.add, [r_a2, r_kf], [r_a2])
                            act(dst[:], a2[:], AF.Sin, [r_a2], [r_dst])
                        P.barrier()
                        ck("rope%d" % pas, cosT=cosT[:], sinT=sinT[:])


                    kT = sb("kT", [128, 4096], BF16, SAt)
                    r_kTpre = Res()
                    r_kTown = [Res() for _ in range(4)]
                    Vaug = sb("Vaug", [128, 32, 130], BF16, SAt)
                    r_Vpre = Res()
                    r_Vown = [Res() for _ in range(16)]
                    qT = sb("qT", [128, 2048], BF16, SAt)
                    r_qT = [Res() for _ in range(4)]
                    rawR = Ring(nc, SAt, "raw", [128, 512], BF16, 2)
                    t1R = Ring(nc, SAt, "t1", [128, 512], F32, 2)
                    t2R = Ring(nc, SAt, "t2", [128, 512], F32, 2)
                    if own:
                        ER = Ring(nc, SAt, "E", [128, 256], BF16, 10)
                        o1R = Ring(nc, SAt, "o1", [128, 128], F32, 2)
                        acS0R = Ring(nc, SAt, "acS0", [128, 2, 132], F32, 2)
                        acS1R = Ring(nc, SAt, "acS1", [128, 2, 132], F32, 2)
                        o2R = Ring(nc, SAt, "o2", [128, 128], F32, 2)
                        onR = Ring(nc, SAt, "on", [128, 128], BF16, 2)
                        astR = Ring(nc, SAt, "ast", [128, 8], F32, 2)
                        oTR = Ring(nc, SAt, "oTst", [128, 2048], BF16, 2)
                    sqo = sb("sqo", [128, 128], BF16, SAt)
                    r_sqo = Res()
                    koff = 2048 if own else 0

                    def rope_block(col0, dst, dst_off, r_dst_list):
                        wb, r_wb = load_wblk(col0, 128)
                        pend = []

                        def stage2(gq, bkA, rbkA, raw, r_raw):
                            bkB, rbkB = bank()
                            mm(bkB[:, :], protb[:], raw[:], True, True, [r_protb, r_raw], [rbkB])
                            t1, r_t1 = t1R.get()
                            t2, r_t2 = t2R.get()
                            ts = slice(gq * 512, (gq + 1) * 512)
                            tt("dve", t1[:], bkA[:, :], cosT[:, ts], ALU.mult, [rbkA, r_cos], [r_t1])
                            tt("dve", t2[:], bkB[:, :], sinT[:, ts], ALU.mult, [rbkB, r_sin], [r_t2])
                            tt("pool", dst[:, dst_off + gq * 512:dst_off + (gq + 1) * 512], t1[:], t2[:], ALU.add, [r_t1, r_t2],
                               [r_dst_list[gq]])

                        for gq in range(4):
                            bkA, rbkA = bank()
                            proj_fm(wb, r_wb, gq * 512, 512, bkA, rbkA)
                            raw, r_raw = rawR.get()
                            act(raw[:], bkA[:, :], AF.Copy, [rbkA], [r_raw])
                            if pend:
                                stage2(*pend.pop(0))
                            pend.append((gq, bkA, rbkA, raw, r_raw))
                        stage2(*pend.pop(0))

                    nset = 2 if own else 1
                    bufs = [dict(kT=kT, r_kTpre=r_kTpre, r_kTown=r_kTown, Vaug=Vaug, r_Vpre=r_Vpre, r_Vown=r_Vown, qT=qT, r_qT=r_qT)]
                    if own:
                        bufs.append(dict(kT=sb("kT2", [128, 4096], BF16, SAt), r_kTpre=Res(), r_kTown=[Res() for _ in range(4)],
                                         Vaug=sb("Vaug2", [128, 32, 130], BF16, SAt), r_Vpre=Res(), r_Vown=[Res() for _ in range(16)],
                                         qT=sb("qT2", [128, 2048], BF16, SAt), r_qT=[Res() for _ in range(4)]))

                    def rope_block_gen(col0, dst, dst_off, r_dst_list):
                        wb, r_wb = load_wblk(col0, 128)
                        pend = []

                        def stage2(gq, bkA, rbkA, raw, r_raw):
                            bkB, rbkB = bank(0, 6)
                            mm(bkB[:, :], protb[:], raw[:], True, True, [r_protb, r_raw], [rbkB])
                            t1, r_t1 = t1R.get()
                            t2, r_t2 = t2R.get()
                            ts = slice(gq * 512, (gq + 1) * 512)
                            tt("dve", t1[:], bkA[:, :], cosT[:, ts], ALU.mult, [rbkA, r_cos], [r_t1])
                            tt("dve", t2[:], bkB[:, :], sinT[:, ts], ALU.mult, [rbkB, r_sin], [r_t2])
                            tt("pool", dst[:, dst_off + gq * 512:dst_off + (gq + 1) * 512], t1[:], t2[:], ALU.add, [r_t1, r_t2],
                               [r_dst_list[gq]])

                        for gq in range(4):
                            bkA, rbkA = bank(0, 6)
                            proj_fm(wb, r_wb, gq * 512, 512, bkA, rbkA)
                            raw, r_raw = rawR.get()
                            act(raw[:], bkA[:, :], AF.Copy, [rbkA], [r_raw])
                            if pend:
                                stage2(*pend.pop(0))
                            pend.append((gq, bkA, rbkA, raw, r_raw))
                            if gq == 1 and own:
                                stage2(*pend.pop(0))
                                yield
                        stage2(*pend.pop(0))
                        yield

                    def prep_gen(hd, B):
                        kT, r_kTpre, r_kTown = B["kT"], B["r_kTpre"], B["r_kTown"]
                        Vaug, r_Vpre, r_Vown = B["Vaug"], B["r_Vpre"], B["r_Vown"]
                        qT, r_qT = B["qT"], B["r_qT"]
                        if own:
                            P.dma("sp", kT[:, 0:2048], kpre_d[hd], writes=[r_kTpre])
                            P.dma("sp", Vaug[:, 0:16, :].rearrange("p t c -> p (t c)"), vpre_d[hd], writes=[r_Vpre])
                        yield from rope_block_gen(6176 + hd * 128, kT, koff, r_kTown)
                        wv, r_wv = load_wblk(7200 + hd * 128, 128)
                        vo = 16 if own else 0
                        for t in range(16):
                            if t % 4 == 0:
                                bk, rbk = bank(0, 6)
                            for kc in range(8):
                                mm(bk[:, (t % 4) * 128:(t % 4 + 1) * 128], hT[:, kc, t * 128:(t + 1) * 128], wv[:, kc, :],
                                   kc == 0, kc == 7, [r_wv, r_hT[t]], [rbk])
                            if t % 4 == 3:
                                src = bk[:, :].rearrange("p (t c) -> p t c", t=4)
                                if own:
                                    act(Vaug[:, vo + t - 3:vo + t + 1, 0:128], src, AF.Copy, [rbk], r_Vown[t - 3:t + 1])
                                else:
                                    act(Vaug[:, t - 3:t + 1, 0:128], src, AF.Copy, [rbk, r_valid], r_Vown[t - 3:t + 1],
                                        scale=valid[:, 0:1])
                            if t == 7 and own:
                                yield
                        if own:
                            ms("pool", Vaug[:, 16:32, 128:130], 1.0, r_Vown)
                        else:
                            cp("pool", Vaug[:, 0:16, 128:130], valid[:, 0:1].unsqueeze(1).to_broadcast([128, 16, 2]), [r_valid],
                               r_Vown)
                        if not own:
                            P.dma("sp", kpre_d[hd], kT[:, 0:2048], reads=r_kTown)
                            P.dma("sp", vpre_d[hd], Vaug[:, 0:16, :].rearrange("p t c -> p (t c)"), reads=r_Vown)
                            return
                        yield
                        yield from rope_block_gen(5152 + hd * 128, qT, 0, r_qT)

                    def sweep(hd, B, inject):
                        kT, r_kTpre, r_kTown = B["kT"], B["r_kTpre"], B["r_kTown"]
                        Vaug, r_Vpre, r_Vown = B["Vaug"], B["r_Vpre"], B["r_Vown"]
                        qT, r_qT = B["qT"], B["r_qT"]
                        oTs, r_oTs = oTR.get()
                        LOOK = 2
                        for QG in range(8):
                            nk = 16 + 2 * QG + 2
                            accb = [(banks[6 + c], bres[6 + c]) for c in range(2)]
                            pend = {}
                            for step in range(nk + LOOK):
                                if step < nk:
                                    kt = step
                                    j = kt - 16
                                    q0 = max(0, j - 2 * QG) if j >= 0 else 0
                                    diag = j >= 2 * QG
                                    rk = [r_kTpre] if j < 0 else [r_kTown[j // 4]]
                                    nq = 256 - q0 * 128
                                    qs = slice(QG * 256 + q0 * 128, QG * 256 + 256)
                                    sb_ = [bank(0, 6) for _ in range(2)]
                                    for c in range(2):
                                        bkS, rbkS = sb_[c]
                                        ps = slice(c * 64, (c + 1) * 64)
                                        mm(bkS[:, 0:nq], kT[ps, kt * 128:(kt + 1) * 128], qT[ps, qs], True, not diag,
                                           rk + [r_qT[QG // 2]], [rbkS])
                                    Es = []
                                    for c in range(2):
                                        bkS, rbkS = sb_[c]
                                        if diag:
                                            mm(bkS[:, 0:128], identb[:], negmb[:], False, True, [r_identb, r_negmb], [rbkS])
                                        E, r_E = ER.get()
                                        act(E[:, 0:nq], bkS[:, 0:nq], AF.Exp, [rbkS], [r_E], scale=0.125)
                                        Es.append((E, r_E))
                                    pend[kt] = (Es, q0)
                                if step >= LOOK:
                                    kt = step - LOOK
                                    Es, q0 = pend.pop(kt)
                                    j = kt - 16
                                    rv = [r_Vpre] if j < 0 else [r_Vown[j]]
                                    for c in range(2):
                                        E, r_E = Es[c]
                                        ab, rab = accb[c]
                                        for qi in range(q0, 2):
                                            last = (j == 2 * QG + qi)
                                            P.op("pe", (lambda e, o=ab[:, qi * 256:qi * 256 + 129], l=E[:, (qi - q0) * 128:(qi - q0 + 1) * 128],
                                                        r=Vaug[:, kt, 0:129], st_=(kt == 0 and qi == 0), sp_=last:
                                                        e.matmul(o, lhsT=l, rhs=r, start=st_, stop=sp_, skip_group_check=True)),
                                                 [r_E] + rv, [rab])
                            as0, r_as0 = acS0R.get()
                            as1, r_as1 = acS1R.get()
                            act(as0[:, :, 0:129], accb[0][0][:, :].rearrange("p (a b) -> p a b", a=2)[:, :, 0:129], AF.Copy,
                                [accb[0][1]], [r_as0])
                            cp("dve", as1[:, :, 0:129], accb[1][0][:, :].rearrange("p (a b) -> p a b", a=2)[:, :, 0:129],
                               [accb[1][1]], [r_as1])
                            for qi in range(2):
                                tq = QG * 2 + qi
                                ra1, ra2 = r_as0, r_as1
                                a1 = as0[:, qi, 0:129]
                                a2 = as1[:, qi, 0:129]
                                ast, r_ast = astR.get()
                                recip(ast[:, 0:1], a1[:, 128:129], [ra1], [r_ast])
                                recip(ast[:, 1:2], a2[:, 128:129], [ra2, r_ast], [r_ast])
                                tt("dve", ast[:, 2:3], ast[:, 1:2], neglam[:], ALU.mult, [r_ast, r_neglam], [r_ast])
                                o1, r_o1 = o1R.get()
                                tsc("dve", o1[:], a1[:, 0:128], ast[:, 0:1], None, ALU.mult, None, [ra1, r_ast], [r_o1])
                                o2, r_o2 = o2R.get()
                                stt("dve", o2[:], a2[:, 0:128], ast[:, 2:3], o1[:], ALU.mult, ALU.add, [ra2, r_ast, r_o1], [r_o2])
                                act(sqo[:], o2[:], AF.Square, [r_o2], [r_sqo, r_ast], accum=ast[:, 3:4])
                                act(ast[:, 4:5], ast[:, 3:4], AF.Ln, [r_ast], [r_ast], bias=EPS, scale=1.0 / 128)
                                act(ast[:, 5:6], ast[:, 4:5], AF.Exp, [r_ast], [r_ast], scale=-0.5)
                                on, r_on = onR.get()
                                tsc("dve", on[:], o2[:], ast[:, 5:6], None, ALU.mult, None, [r_o2, r_ast], [r_on])
                                bkT, rbkT = bank(0, 6)
                                bv = bkT[:].bitcast(BF16)
                                tr(bv[:, 0:128], on[:], identb[:], [r_on, r_identb], [rbkT])
                                act(oTs[:, tq * 128:(tq + 1) * 128], bv[:, 0:128], AF.Copy, [rbkT, r_sublnw], [r_oTs],
                                    scale=sublnw[:, 0:1])
                            if inject is not None:
                                next(inject, None)

                        P.dma("sp", oT_d[hd], oTs[:], reads=[r_oTs])

                    def gate_gen():
                        gstR = Ring(nc, SAt, "gst", [128, 2048], BF16, 2)
                        geR = Ring(nc, SAt, "ge", [128, 512], F32, 2)
                        for blk in range(16):
                            wb, r_wb = load_wblk(8224 + blk * 128, 128)
                            gst, r_gst = gstR.get()
                            for gq in range(4):
                                bk, rbk = bank(0, 6)
                                proj_fm(wb, r_wb, gq * 512, 512, bk, rbk)
                                ge, r_ge = geR.get()
                                act(ge[:], bk[:, :], AF.Exp, [rbk], [r_ge], scale=-1.0)
                                tsc("dve", ge[:], ge[:], 1.0, None, ALU.add, None, [r_ge], [r_ge])
                                recip(ge[:], ge[:], [r_ge], [r_ge])
                                cp("dve", gst[:, gq * 512:(gq + 1) * 512], ge[:], [r_ge], [r_gst])
                                if gq == 1:
                                    yield
                            P.dma("sp", gT_d[blk], gst[:], reads=[r_gst])
                            yield

                    if not own:
                        def allprep():
                            for hd in range(8):
                                yield from prep_gen(hd, bufs[0])
                        pg = allprep()
                        cnt_ = [0]

                        def inj():
                            cnt_[0] += 1
                            if cnt_[0] % 6 == 0:
                                next(pg, None)
                        if ssd_call is not None:
                            ssd_call(inj)
                        for _ in pg:
                            pass
                    else:
                        gg = gate_gen()

                        def both(g1_):
                            while True:
                                if g1_ is not None:
                                    next(g1_, None)
                                next(gg, None)
                                yield

                        for _ in prep_gen(0, bufs[0]):
                            pass
                        for hd in range(8):
                            g = prep_gen(hd + 1, bufs[(hd + 1) % 2]) if hd < 7 else None
                            sweep(hd, bufs[hd % 2], both(g))
                            if g is not None:
                                for _ in g:
                                    pass
                        for _ in gg:
                            pass
                    P.barrier()
                    ck("att%d" % pas)

                if own:
                    with contextlib.ExitStack() as SS_:
                        run_ssd(SS_, None)
                    with contextlib.ExitStack() as SAt_:
                        run_att(SAt_, None)
                else:
                    def ssd_scoped(inj):
                        with contextlib.ExitStack() as SS_:
                            run_ssd(SS_, inj)
                    with contextlib.ExitStack() as SAt_:
                        run_att(SAt_, ssd_scoped)
                P.barrier()

        with contextlib.ExitStack() as SEF:
            h2T = sb("h2T", [128, 8, 2048], BF16, SEF)
            r_h2T = [Res() for _ in range(16)]
            with contextlib.ExitStack() as SCD:
                mergedT = sb("mergedT", [128, 8, 2048], BF16, SCD)
                r_mg = [Res() for _ in range(8)]
                with contextlib.ExitStack() as SC:
                    ynR_ = Ring(nc, SC, "ynTa", [128, 16, 512], BF16, 2)
                    oTR_ = Ring(nc, SC, "oTa", [128, 8, 512], BF16, 2)
                    wosS = Ring(nc, SC, "wosS", [128, 16, 128], F32, 2)
                    wosB = Ring(nc, SC, "wosB", [128, 16, 128], BF16, 2)
                    woaS = Ring(nc, SC, "woaS", [128, 8, 128], F32, 2)
                    woaB = Ring(nc, SC, "woaB", [128, 8, 128], BF16, 2)
                    gsR = Ring(nc, SC, "gs", [128, 512], BF16, 2)
                    gaR = Ring(nc, SC, "ga", [128, 512], BF16, 2)
                    m1R = Ring(nc, SC, "m1", [128, 512], F32, 2)
                    m2R = Ring(nc, SC, "m2", [128, 512], F32, 2)
                    for gq in range(4):
                        ts = slice(gq * 512, (gq + 1) * 512)
                        ynTa, r_yn = ynR_.get()
                        oTa, r_oTa = oTR_.get()
                        P.dma("sp", ynTa[:], ynT_d[:, :, ts].rearrange("k p t -> p k t"), writes=[r_yn])
                        P.dma("sp", oTa[:], oT_d[:, :, ts].rearrange("k p t -> p k t"), writes=[r_oTa])
                        for f in range(8):
                            ws, r_ws = wosS.get()
                            wb, r_wb = wosB.get()
                            P.dma("sp", ws[:], w_o_ssd[:, f * 128:(f + 1) * 128].rearrange("(c p) n -> p c n", p=128), writes=[r_ws])
                            tt("dve", wb[:], ws[:], ssdn[:].unsqueeze(2).to_broadcast([128, 16, 128]), ALU.mult, [r_ws, r_ssdn], [r_wb])
                            as_, r_as = woaS.get()
                            ab_, r_ab = woaB.get()
                            P.dma("sp", as_[:], w_o_attn[:, f * 128:(f + 1) * 128].rearrange("(c p) n -> p c n", p=128), writes=[r_as])
                            act(ab_[:], as_[:], AF.Copy, [r_as], [r_ab])
                            gs, r_gs = gsR.get()
                            ga, r_ga = gaR.get()
                            P.dma("sp", gs[:], gT_d[f, :, ts], writes=[r_gs])
                            P.dma("sp", ga[:], gT_d[8 + f, :, ts], writes=[r_ga])
                            bkA, rbkA = bank()
                            for kt in range(16):
                                mm(bkA[:, :], wb[:, kt, :], ynTa[:, kt, :], kt == 0, kt == 15, [r_wb, r_yn], [rbkA])
                            bkB, rbkB = bank()
                            for kt in range(8):
                                mm(bkB[:, :], ab_[:, kt, :], oTa[:, kt, :], kt == 0, kt == 7, [r_ab, r_oTa], [rbkB])
                            m1, r_m1 = m1R.get()
                            m2, r_m2 = m2R.get()
                            tt("dve", m1[:], bkA[:, :], gs[:], ALU.mult, [rbkA, r_gs], [r_m1])
                            tt("dve", m2[:], bkB[:, :], ga[:], ALU.mult, [rbkB, r_ga], [r_m2])
                            tt("pool", mergedT[:, f, ts], m1[:], m2[:], ALU.add, [r_m1, r_m2], [r_mg[f]])
                    P.barrier()
                    ck("C", mergedT=mergedT[:])

                with contextlib.ExitStack() as SD:
                    woutB = sb("woutB", [128, 8, 1024], BF16, SD)
                    r_wout = Res()
                    wS = Ring(nc, SD, "woutS", [128, 8, 256], F32, 2)
                    for q in range(4):
                        s_t, s_r = wS.get()
                        P.dma("sp", s_t[:], w_out[:, q * 256:(q + 1) * 256].rearrange("(c p) n -> p c n", p=128), writes=[s_r])
                        cp("dve", woutB[:, :, q * 256:(q + 1) * 256], s_t[:], [s_r], [r_wout])
                    xR = Ring(nc, SD, "xtd", [128, 1024], F32, 2)
                    x1R = Ring(nc, SD, "x1", [128, 1024], F32, 3)
                    tmR = Ring(nc, SD, "tm", [128, 1024], F32, 2)
                    xnR = Ring(nc, SD, "xnd", [128, 1024], BF16, 2)
                    sqj = sb("sqjd", [128, 1024], BF16, SD)
                    r_sqj = Res()
                    statR = Ring(nc, SD, "statd", [128, 8], F32, 2)
                    stat2R = Ring(nc, SD, "stat2d", [128, 4], F32, 2)
                    dpend = []
                    pendA2 = None
                    for t in range(16):
                        tks = slice(t * 128, (t + 1) * 128)
                        xt, r_xt = xR.get()
                        P.dma("sp", xt[:], xin[2048 + t * 128:2048 + (t + 1) * 128, :], writes=[r_xt])
                        bk0, rbk0 = bank()
                        bk1, rbk1 = bank()
                        for kc in range(8):
                            mm(bk0[:, :], mergedT[:, kc, tks], woutB[:, kc, 0:512], kc == 0, kc == 7, [r_mg[kc], r_wout], [rbk0])
                        for kc in range(8):
                            mm(bk1[:, :], mergedT[:, kc, tks], woutB[:, kc, 512:1024], kc == 0, kc == 7, [r_mg[kc], r_wout], [rbk1])
                        while dpend:
                            dpend.pop(0)()
                        stat, r_stat = statR.get()
                        act(sqj[:, 0:512], bk0[:, :], AF.Square, [rbk0], [r_sqj, r_stat], accum=stat[:, 0:1])
                        act(sqj[:, 512:1024], bk1[:, :], AF.Square, [rbk1, r_stat], [r_sqj, r_stat], accum=stat[:, 1:2])
                        tt("dve", stat[:, 2:3], stat[:, 0:1], stat[:, 1:2], ALU.add, [r_stat], [r_stat])
                        act(stat[:, 3:4], stat[:, 2:3], AF.Ln, [r_stat], [r_stat], bias=EPS, scale=1.0 / 1024)
                        act(stat[:, 4:5], stat[:, 3:4], AF.Exp, [r_stat], [r_stat], scale=-0.5)
                        tm, r_tm = tmR.get()
                        stt("dve", tm[:, 0:512], bk0[:, :], stat[:, 4:5], g1n[:, 0:512], ALU.mult, ALU.mult, [rbk0, r_stat, r_g1n], [r_tm])
                        stt("dve", tm[:, 512:1024], bk1[:, :], stat[:, 4:5], g1n[:, 512:1024], ALU.mult, ALU.mult,
                            [rbk1, r_stat, r_g1n], [r_tm])
                        x1, r_x1 = x1R.get()
                        tt("pool", x1[:], tm[:], xt[:], ALU.add, [r_tm, r_xt], [r_x1])
                        P.dma("sp", x1_d[t * 128:(t + 1) * 128, :], x1[:], reads=[r_x1])
                        if pendA2 is not None:
                            px1, pr_x1, pt = pendA2
                            stat2, r_stat2 = stat2R.get()
                            norm_transpose(px1, pr_x1, gsc2, sh2, h2T, pt * 128, r_h2T[pt], sqj, r_sqj, stat2, r_stat2, xnR, defer=dpend)
                        pendA2 = (x1, r_x1, t)
                    px1, pr_x1, pt = pendA2
                    stat2, r_stat2 = stat2R.get()
                    norm_transpose(px1, pr_x1, gsc2, sh2, h2T, pt * 128, r_h2T[pt], sqj, r_sqj, stat2, r_stat2, xnR, defer=dpend)
                    while dpend:
                        dpend.pop(0)()
                    P.barrier()
                    ck("D", h2T=h2T[:])

            with contextlib.ExitStack() as SF:
                wdB = sb("wdB", [128, 22, 1024], BF16, SF)
                r_wd = Res()
                wdS = Ring(nc, SF, "wdS", [128, 1, 1024], F32, 2)
                wdv = w_down.rearrange("(c p) n -> p c n", p=128)
                for q in range(22):
                    s_t, s_r = wdS.get()
                    P.dma("sp", s_t[:], wdv[:, q:q + 1, :], writes=[s_r])
                    cp("dve", wdB[:, q:q + 1, :], s_t[:], [s_r], [r_wd])
                aT = sb("aT", [128, 22, 1024], BF16, SF)
                wgS = Ring(nc, SF, "wgS", [128, 8, 128], F32, 2)
                wgB = Ring(nc, SF, "wgB", [128, 8, 128], BF16, 2)
                wuS = Ring(nc, SF, "wuS", [128, 8, 128], F32, 2)
                wuB = Ring(nc, SF, "wuB", [128, 8, 128], BF16, 2)
                sgR = Ring(nc, SF, "sg", [128, 512], F32, 2)
                x1R = Ring(nc, SF, "x1f", [128, 1024], F32, 2)
                tmR = Ring(nc, SF, "tmf", [128, 1024], F32, 2)
                sqj = sb("sqjf", [128, 1024], BF16, SF)
                r_sqj = Res()
                statR = Ring(nc, SF, "statf", [128, 8], F32, 2)
                outs = []
                for half in range(2):
                    r_aT = [Res() for _ in range(22)]
                    for j in range(22):
                        gs_, r_gs_ = wgS.get()
                        gb_, r_gb_ = wgB.get()
                        us_, r_us_ = wuS.get()
                        ub_, r_ub_ = wuB.get()
                        P.dma("sp", gs_[:], w_gate[:, j * 128:(j + 1) * 128].rearrange("(c p) n -> p c n", p=128), writes=[r_gs_])
                        cp("pool", gb_[:], gs_[:], [r_gs_], [r_gb_])
                        P.dma("sp", us_[:], w_up[:, j * 128:(j + 1) * 128].rearrange("(c p) n -> p c n", p=128), writes=[r_us_])
                        act(ub_[:], us_[:], AF.Copy, [r_us_], [r_ub_])
                        for gq in range(2):
                            ts = slice(half * 1024 + gq * 512, half * 1024 + (gq + 1) * 512)
                            rh = r_h2T[half * 8 + gq * 4:half * 8 + (gq + 1) * 4]
                            bkG, rbkG = bank()
                            bkU, rbkU = bank()
                            for kc in range(8):
                                mm(bkG[:, :], gb_[:, kc, :], h2T[:, kc, ts], kc == 0, kc == 7, [r_gb_] + rh, [rbkG])
                            for kc in range(8):
                                mm(bkU[:, :], ub_[:, kc, :], h2T[:, kc, ts], kc == 0, kc == 7, [r_ub_] + rh, [rbkU])
                            sg, r_sg = sgR.get()
                            act(sg[:], bkG[:, :], AF.Silu, [rbkG], [r_sg])
                            tt("dve", aT[:, j, gq * 512:(gq + 1) * 512], bkU[:, :], sg[:], ALU.mult, [rbkU, r_sg], [r_aT[j]])
                    ck("E%d" % half, aT=aT[:])
                    for tl in range(8):
                        t = half * 8 + tl
                        tks = slice(tl * 128, (tl + 1) * 128)
                        x1, r_x1 = x1R.get()
                        P.dma("sp", x1[:], x1_d[t * 128:(t + 1) * 128, :], writes=[r_x1])
                        bk0, rbk0 = bank()
                        bk1, rbk1 = bank()
                        for j in range(22):
                            mm(bk0[:, :], aT[:, j, tks], wdB[:, j, 0:512], j == 0, j == 21, [r_aT[j], r_wd], [rbk0])
                        for j in range(22):
                            mm(bk1[:, :], aT[:, j, tks], wdB[:, j, 512:1024], j == 0, j == 21, [r_aT[j], r_wd], [rbk1])
                        stat, r_stat = statR.get()
                        act(sqj[:, 0:512], bk0[:, :], AF.Square, [rbk0], [r_sqj, r_stat], accum=stat[:, 0:1])
                        act(sqj[:, 512:1024], bk1[:, :], AF.Square, [rbk1, r_stat], [r_sqj, r_stat], accum=stat[:, 1:2])
                        tt("dve", stat[:, 2:3], stat[:, 0:1], stat[:, 1:2], ALU.add, [r_stat], [r_stat])
                        act(stat[:, 3:4], stat[:, 2:3], AF.Ln, [r_stat], [r_stat], bias=EPS, scale=1.0 / 1024)
                        act(stat[:, 4:5], stat[:, 3:4], AF.Exp, [r_stat], [r_stat], scale=-0.5)
                        tm, r_tm = tmR.get()
                        stt("dve", tm[:, 0:512], bk0[:, :], stat[:, 4:5], g2n[:, 0:512], ALU.mult, ALU.mult, [rbk0, r_stat, r_g2n], [r_tm])
                        stt("dve", tm[:, 512:1024], bk1[:, :], stat[:, 4:5], g2n[:, 512:1024], ALU.mult, ALU.mult,
                            [rbk1, r_stat, r_g2n], [r_tm])
                        tt("pool", x1[:], tm[:], x1[:], ALU.add, [r_tm, r_x1], [r_x1])
                        outs.append(P.dma("sp", out[t * 128:(t + 1) * 128, :], x1[:], reads=[r_x1]))
                    P.barrier()
                    ck("F%d" % half)
                P.emit(final_waits=outs)


_CACHE = {}


def _consts():
    ident = np.eye(128, dtype=np.float32)
    prot = np.zeros((128, 128), np.float32)
    for blk in range(2):
        for i in range(32):
            dlo = blk * 64 + i
            dhi = blk * 64 + i + 32
            prot[dhi, dlo] = -1.0
            prot[dlo, dhi] = 1.0
    k = np.arange(128)[:, None]
    l = np.arange(128)[None, :]
    negm = np.where(l < k, NEG, 0.0).astype(np.float32)
    utri = (k <= l).astype(np.float32)
    sel = np.zeros((8, 8, 128), np.float32)
    for h in range(8):
        sel[h, h, :] = 1.0
    invf = (1.0 / (10000.0 ** (np.arange(0, 64, 2, dtype=np.float32) / 64.0))).astype(np.float32)
    invf = np.tile(invf, 4).reshape(128, 1).astype(np.float32)
    return dict(ident=ident, prot=prot, negm=negm, utri=utri, sel=sel, invf=invf)


def _in_maps(x, c, positions, w_ada, b_ada, norm_pre_mix, norm_post_mix, norm_pre_ffn, norm_post_ffn,
           w_in, conv_w, conv_b, dt_bias, a_log, d_skip, ssd_norm, w_o_ssd,
           lambda_q1, lambda_k1, lambda_q2, lambda_k2, subln, w_o_attn, w_out,
           w_gate, w_up, w_down):
    f32 = lambda a: np.ascontiguousarray(np.asarray(a), dtype=np.float32)
    x = f32(x)
    c = f32(c)
    positions = np.ascontiguousarray(np.asarray(positions), dtype=np.int32)
    col = lambda v, n: np.ascontiguousarray(f32(v).reshape(n, 128).T)
    rep = lambda v: np.ascontiguousarray(np.broadcast_to(f32(v).reshape(1, -1), (128, f32(v).size)))
    b_ada0 = f32(b_ada)[0]
    shared = dict(
        w_ada=f32(w_ada)[0], bada_col=col(b_ada0, 48),
        bada_g=np.ascontiguousarray(np.stack([rep(b_ada0[2048:3072]), rep(b_ada0[5120:6144])], axis=1)),
        npre_mix=col(norm_pre_mix, 8), npre_ffn=col(norm_pre_ffn, 8),
        npost_mix=rep(norm_post_mix), npost_ffn=rep(norm_post_ffn),
        w_in=f32(w_in)[0],
        convw=np.ascontiguousarray(f32(conv_w)[0].reshape(24, 128, 4).transpose(1, 0, 2)),
        convb=col(conv_b, 24),
        dtb=rep(dt_bias), alog=rep(a_log), dskip=rep(d_skip), ssdn=col(ssd_norm, 16),
        w_o_ssd=f32(w_o_ssd)[0],
        lam=np.ascontiguousarray(np.stack([rep(lambda_q1), rep(lambda_k1), rep(lambda_q2), rep(lambda_k2)], axis=1)),
        subln=np.ascontiguousarray(f32(subln).reshape(128, 1)),
        w_o_attn=f32(w_o_attn)[0], w_out=f32(w_out)[0], w_gate=f32(w_gate)[0], w_up=f32(w_up)[0],
        w_down=f32(w_down)[0],
    )
    shared.update(_consts())
    in_maps = []
    for core in range(8):
        b, half = core // 2, core % 2
        m = dict(shared)
        if half == 0:
            xin = np.concatenate([np.zeros((2048, 1024), np.float32), x[b, 0:2048]], axis=0)
            pos = np.concatenate([np.zeros((2048,), np.int32), positions[b, 0:2048]], axis=0)
        else:
            xin = x[b]
            pos = positions[b]
        m["xin"] = np.ascontiguousarray(xin)
        m["pos"] = np.ascontiguousarray(np.broadcast_to(pos.reshape(1, 4096), (128, 4096)))
        m["valid"] = np.full((128, 1), float(half), np.float32)
        m["ccol"] = col(c[b], 8)
        in_maps.append(m)
    return in_maps


def kernel(**inputs):
    in_maps = _in_maps(**inputs)
    if "nc" not in _CACHE:
        _CACHE["nc"] = build()
    res = run_bass_kernel_spmd(_CACHE["nc"], in_maps, core_ids=list(range(8)))
    outp = np.empty((4, 4096, 1024), np.float32)
    for core in range(8):
        b, half = core // 2, core % 2
        outp[b, half * 2048:(half + 1) * 2048] = res.results[core]["out"]
    return outp
```

```python
import math
import contextlib
import numpy as np
import concourse.bass as bass
import concourse.mybir as mybir
from concourse.bass_utils import run_bass_kernel_spmd

F32 = mybir.dt.float32
BF16 = mybir.dt.bfloat16
I32 = mybir.dt.int32
ALU = mybir.AluOpType
AF = mybir.ActivationFunctionType

ENGS = ("pe", "act", "dve", "pool", "sp")
EPS = 1e-6
NEG = -30000.0
LAMBDA_INIT = 0.8 - 0.6 * math.exp(0.0)


class Res:
    __slots__ = ("name", "w", "r", "excl")

    def __init__(self, name="", excl=False):
        self.name = name
        self.w = None
        self.r = []
        self.excl = excl


class Op:
    __slots__ = ("eng", "fn", "deps", "sig", "cnt", "dma", "dslot", "dval", "bar")

    def __init__(self, eng, fn, dma):
        self.eng = eng
        self.fn = fn
        self.deps = []
        self.sig = False
        self.cnt = 0
        self.dma = dma
        self.dslot = None
        self.dval = 0
        self.bar = False


class Prog:
    KD = 8

    def __init__(self, nc):
        self.nc = nc
        self.ops = []
        self.per = {e: [] for e in ENGS}

    def _add(self, eng, fn, reads, writes, dma):
        o = Op(eng, fn, dma)
        deps = set()
        xr = [r for r in reads if r.excl]
        if xr:
            reads = [r for r in reads if not r.excl]
            writes = list(writes) + [r for r in xr if r not in writes]
        for r in reads:
            if r.w is not None:
                deps.add(r.w)
        for r in writes:
            if r.w is not None:
                deps.add(r.w)
            for x in r.r:
                deps.add(x)
        for r in reads:
            r.r.append(o)
        for r in writes:
            r.w = o
            r.r = []
        deps.discard(o)
        o.deps = list(deps)
        self.ops.append(o)
        self.per[eng].append(o)
        return o

    def op(self, eng, fn, reads=(), writes=()):
        return self._add(eng, fn, reads, writes, False)

    def dma(self, eng, out, in_, reads=(), writes=()):
        return self._add(eng, lambda e: e.dma_start(out=out, in_=in_), reads, writes, True)

    def barrier(self):
        deps = []
        for e in ENGS:
            real = [o for o in self.per[e] if not o.bar and not o.dma]
            if real:
                deps.append(real[-1])
            deps += [o for o in self.per[e] if o.dma][-self.KD:]
        for e in ENGS:
            o = Op(e, None, False)
            o.bar = True
            o.deps = list(deps)
            self.ops.append(o)
            self.per[e].append(o)

    def emit(self, final_waits=()):
        nc = self.nc
        for o in self.ops:
            for d in o.deps:
                if d.dma:
                    continue
                if d.eng == o.eng and o.eng == "pe":
                    continue
                d.sig = True
        ndma = {e: 0 for e in ENGS}
        for e in ENGS:
            c = 0
            for o in self.per[e]:
                if o.dma:
                    i = ndma[e]
                    ndma[e] += 1
                    o.dslot = (e, i % self.KD)
                    o.dval = 16 * (i // self.KD + 1)
                elif o.sig:
                    c += 1
                    o.cnt = c
        with contextlib.ExitStack() as st:
            sem = {e: st.enter_context(nc.semaphore("s_" + e)) for e in ENGS}
            dsem = {}
            for e in ENGS:
                for k in range(min(self.KD, ndma[e])):
                    dsem[(e, k)] = st.enter_context(nc.semaphore("d_%s%d" % (e, k)))
            block = st.enter_context(nc.Block())
            engobj = {"pe": block.tensor, "act": block.scalar, "dve": block.vector,
                      "pool": block.gpsimd, "sp": block.sync}

            def body(e):
                def f(eng):
                    seen = {}
                    for o in self.per[e]:
                        for d in o.deps:
                            if d.dma:
                                key = ("d",) + d.dslot
                                if seen.get(key, 0) < d.dval:
                                    eng.wait_ge(dsem[d.dslot], d.dval)
                                    seen[key] = d.dval
                            else:
                                if d.eng == e and e == "pe":
                                    continue
                                if seen.get(d.eng, 0) < d.cnt:
                                    eng.wait_ge(sem[d.eng], d.cnt)
                                    seen[d.eng] = d.cnt
                        if o.bar:
                            continue
                        if o.dma and o.dval > 16:
                            key = ("d",) + o.dslot
                            if seen.get(key, 0) < o.dval - 16:
                                eng.wait_ge(dsem[o.dslot], o.dval - 16)
                                seen[key] = o.dval - 16
                        ins = o.fn(eng)
                        if o.dma:
                            ins.then_inc(dsem[o.dslot], 16)
                        elif o.sig:
                            ins.then_inc(sem[e], 1)
                    if e == "sp":
                        for o in final_waits:
                            eng.wait_ge(dsem[o.dslot], o.dval)
                return f

            for e in ENGS:
                engobj[e](body(e))


class Ring:
    U = 0

    def __init__(self, nc, st, name, shape, dt, n):
        Ring.U += 1
        self.t = [st.enter_context(nc.sbuf_tensor("r%d_%s%d" % (Ring.U, name, i), shape, dt)) for i in range(n)]
        self.r = [Res(name) for _ in range(n)]
        self.i = 0

    def get(self):
        k = self.i % len(self.t)
        self.i += 1
        return self.t[k], self.r[k]


class _Stop(Exception):
    pass


def build(stop=None, debug=False):
    nc = bass.Bass("TRN2", target_bir_lowering=False)

    def din(name, shape, dt=F32):
        return nc.dram_tensor(name, shape, dt, kind="ExternalInput").ap()

    def dsc(name, shape, dt):
        return nc.dram_tensor(name, shape, dt, kind="ExternalOutput" if debug else "Internal").ap()

    xin = din("xin", [4096, 1024])
    pos_d = din("pos", [128, 4096], I32)
    invf_d = din("invf", [128, 1])
    valid_d = din("valid", [128, 1])
    ccol_d = din("ccol", [128, 8])
    w_ada = din("w_ada", [1024, 6144])
    bada_col_d = din("bada_col", [128, 48])
    bada_g_d = din("bada_g", [128, 2, 1024])
    npre_mix_d = din("npre_mix", [128, 8])
    npre_ffn_d = din("npre_ffn", [128, 8])
    npost_mix_d = din("npost_mix", [128, 1024])
    npost_ffn_d = din("npost_ffn", [128, 1024])
    w_in = din("w_in", [1024, 10272])
    convw_d = din("convw", [128, 24, 4])
    convb_d = din("convb", [128, 24])
    dtb_d = din("dtb", [128, 32])
    alog_d = din("alog", [128, 32])
    dskip_d = din("dskip", [128, 32])
    ssdn_d = din("ssdn", [128, 16])
    w_o_ssd = din("w_o_ssd", [2048, 1024])
    lam_d = din("lam", [128, 4, 64])
    subln_d = din("subln", [128, 1])
    w_o_attn = din("w_o_attn", [1024, 1024])
    w_out = din("w_out", [1024, 1024])
    w_gate = din("w_gate", [1024, 2816])
    w_up = din("w_up", [1024, 2816])
    w_down = din("w_down", [2816, 1024])
    ident_d = din("ident", [128, 128])
    prot_d = din("prot", [128, 128])
    negm_d = din("negm", [128, 128])
    utri_d = din("utri", [128, 128])
    sel_d = din("sel", [8, 8, 128])
    out = nc.dram_tensor("out", [2048, 1024], F32, kind="ExternalOutput").ap()

    kpre_d = dsc("kpre", [8, 128, 2048], BF16)
    vpre_d = dsc("vpre", [8, 128, 16 * 130], BF16)
    ynT_d = dsc("ynT", [16, 128, 2048], BF16)
    gT_d = dsc("gT", [16, 128, 2048], BF16)
    x1_d = dsc("x1", [2048, 1024], F32)
    oT_d = dsc("oTd", [8, 128, 2048], BF16)

    P = Prog(nc)
    dbg_ops = []

    def ck(name, **dumps):
        if stop != name:
            return
        P.barrier()
        for k, ap in dumps.items():
            d = nc.dram_tensor("dbg_" + k, list(ap.shape), ap.dtype, kind="ExternalOutput").ap()
            dbg_ops.append(P.dma("sp", d, ap))
        raise _Stop()

    try:
        _build_body(nc, P, ck, locals())
    except _Stop:
        P.emit(final_waits=dbg_ops)
    return nc


def _build_body(nc, P, ck, L):
    (xin, pos_d, invf_d, valid_d, ccol_d, w_ada, bada_col_d, bada_g_d, npre_mix_d, npre_ffn_d, npost_mix_d, npost_ffn_d,
     w_in, convw_d, convb_d, dtb_d, alog_d, dskip_d, ssdn_d, w_o_ssd, lam_d, subln_d, w_o_attn, w_out, w_gate, w_up, w_down,
     ident_d, prot_d, negm_d, utri_d, sel_d, out, kpre_d, vpre_d, ynT_d, gT_d, x1_d, oT_d) = [L[k] for k in (
        "xin pos_d invf_d valid_d ccol_d w_ada bada_col_d bada_g_d npre_mix_d npre_ffn_d npost_mix_d npost_ffn_d "
        "w_in convw_d convb_d dtb_d alog_d dskip_d ssdn_d w_o_ssd lam_d subln_d w_o_attn w_out w_gate w_up w_down "
        "ident_d prot_d negm_d utri_d sel_d out kpre_d vpre_d ynT_d gT_d x1_d oT_d").split()]
    with contextlib.ExitStack() as G:
        uniq = [0]

        def sb(name, shape, dt, st=G):
            uniq[0] += 1
            return st.enter_context(nc.sbuf_tensor("s%d_%s" % (uniq[0], name), shape, dt))

        banks = [G.enter_context(nc.psum_tensor("bk%d" % i, [128, 512], F32)) for i in range(8)]
        bres = [Res("bk%d" % i, excl=True) for i in range(8)]
        bank_i = [0]

        def bank(lo=0, hi=8):
            k = lo + bank_i[0] % (hi - lo)
            bank_i[0] += 1
            return banks[k], bres[k]

        def mm(o, lhsT, rhs, start, stop, reads, writes, sgc=False):
            if sgc:
                P.op("pe", lambda e: e.matmul(o, lhsT=lhsT, rhs=rhs, start=start, stop=stop, skip_group_check=True), reads, writes)
            else:
                P.op("pe", lambda e: e.matmul(o, lhsT=lhsT, rhs=rhs, start=start, stop=stop), reads, writes)

        def tr(o, i, ident, reads, writes):
            P.op("pe", lambda e: e.transpose(out=o, in_=i, identity=ident), reads, writes)

        def act(o, i, func, reads, writes, bias=None, scale=None, accum=None):
            kw = {}
            if bias is not None:
                kw["bias"] = bias
            if scale is not None:
                kw["scale"] = scale
            if accum is not None:
                kw["accum_out"] = accum
            P.op("act", lambda e: e.activation(out=o, in_=i, func=func, **kw), reads, writes)

        def tsc(eng, o, i, s1, s2, op0, op1, reads, writes):
            if s2 is None:
                P.op(eng, lambda e: e.tensor_scalar(out=o, in0=i, scalar1=s1, scalar2=None, op0=op0), reads, writes)
            else:
                P.op(eng, lambda e: e.tensor_scalar(out=o, in0=i, scalar1=s1, scalar2=s2, op0=op0, op1=op1), reads, writes)

        def tt(eng, o, a, b, op, reads, writes):
            P.op(eng, lambda e: e.tensor_tensor(out=o, in0=a, in1=b, op=op), reads, writes)

        def stt(eng, o, a, s, b, op0, op1, reads, writes):
            P.op(eng, lambda e: e.scalar_tensor_tensor(out=o, in0=a, scalar=s, in1=b, op0=op0, op1=op1), reads, writes)

        def cp(eng, o, i, reads, writes):
            P.op(eng, lambda e: e.tensor_copy(out=o, in_=i), reads, writes)

        def ms(eng, o, v, writes):
            P.op(eng, lambda e: e.memset(o, v), (), writes)

        def recip(o, i, reads, writes):
            P.op("dve", lambda e: e.reciprocal(out=o, in_=i), reads, writes)

        def load(name, src, shape, dt=F32, st=G):
            t = sb(name, shape, dt, st)
            r = Res(name)
            P.dma("sp", t[:], src, writes=[r])
            return t, r

        identf, r_identf = load("identf", ident_d, [128, 128])
        protf, r_protf = load("protf", prot_d, [128, 128])
        negmf, r_negmf = load("negmf", negm_d, [128, 128])
        utri, r_utri = load("utri", utri_d, [128, 128])
        valid, r_valid = load("valid", valid_d, [128, 1])
        invf, r_invf = load("invf", invf_d, [128, 1])
        ccol, r_ccol = load("ccol", ccol_d, [128, 8])
        bada_col, r_bada_col = load("bada_col", bada_col_d, [128, 48])
        npre_mix, r_npre_mix = load("npre_mix", npre_mix_d, [128, 8])
        npre_ffn, r_npre_ffn = load("npre_ffn", npre_ffn_d, [128, 8])
        convw, r_convw = load("convw", convw_d, [128, 24, 4])
        convb, r_convb = load("convb", convb_d, [128, 24])
        dtb, r_dtb = load("dtb", dtb_d, [128, 32])
        alog, r_alog = load("alog", alog_d, [128, 32])
        dskip, r_dskip = load("dskip", dskip_d, [128, 32])
        ssdn, r_ssdn = load("ssdn", ssdn_d, [128, 16])
        lamt, r_lamt = load("lamt", lam_d, [128, 4, 64])
        subln, r_subln = load("subln", subln_d, [128, 1])

        identb = sb("identb", [128, 128], BF16)
        r_identb = Res()
        cp("pool", identb[:], identf[:], [r_identf], [r_identb])
        protb = sb("protb", [128, 128], BF16)
        r_protb = Res()
        cp("pool", protb[:], protf[:], [r_protf], [r_protb])
        negmb = sb("negmb", [128, 128], BF16)
        r_negmb = Res()
        cp("pool", negmb[:], negmf[:], [r_negmf], [r_negmb])
        onesf = sb("onesf", [128, 128], F32)
        r_onesf = Res()
        ms("pool", onesf[:], 1.0, [r_onesf])
        Arep = sb("Arep", [128, 32], F32)
        r_Arep = Res()
        act(Arep[:], alog[:], AF.Exp, [r_alog], [r_Arep])
        tsc("dve", Arep[:], Arep[:], -1.0, None, ALU.mult, None, [r_Arep], [r_Arep])
        lprod = sb("lprod", [128, 2, 64], F32)
        lsum = sb("lsum", [128, 2], F32)
        neglam = sb("neglam", [128, 1], F32)
        r_lam = Res()
        r_neglam = Res()
        tt("dve", lprod[:, 0, :], lamt[:, 0, :], lamt[:, 1, :], ALU.mult, [r_lamt], [r_lam])
        tt("dve", lprod[:, 1, :], lamt[:, 2, :], lamt[:, 3, :], ALU.mult, [r_lamt, r_lam], [r_lam])
        P.op("dve", lambda e: e.reduce_sum(out=lsum[:], in_=lprod[:], axis=mybir.AxisListType.X), [r_lam], [r_lam])
        act(lsum[:], lsum[:], AF.Exp, [r_lam], [r_lam])
        tt("dve", neglam[:], lsum[:, 1:2], lsum[:, 0:1], ALU.subtract, [r_lam], [r_neglam])
        tsc("dve", neglam[:], neglam[:], -LAMBDA_INIT, None, ALU.add, None, [r_neglam], [r_neglam])
        sublnw = sb("sublnw", [128, 1], F32)
        r_sublnw = Res()
        tsc("dve", sublnw[:], subln[:], 1.0 - LAMBDA_INIT, None, ALU.mult, None, [r_subln], [r_sublnw])

        state0 = sb("state0", [128, 4, 512], F32)
        r_state0 = [Res() for _ in range(4)]
        utail = sb("utail", [128, 24, 3], F32)
        r_utail = [Res() for _ in range(24)]
        modcol = sb("modcol", [128, 48], F32)
        r_modcol = Res()
        g1n = sb("g1n", [128, 1024], F32)
        g2n = sb("g2n", [128, 1024], F32)
        r_g1n = Res()
        r_g2n = Res()
        gsc1 = sb("gsc1", [128, 8], F32)
        gsc2 = sb("gsc2", [128, 8], F32)
        r_gsc = Res()

        with contextlib.ExitStack() as S0:
            cond = sb("cond", [128, 8], F32, S0)
            r_cond = Res()
            act(cond[:], ccol[:], AF.Silu, [r_ccol], [r_cond])
            condb = sb("condb", [128, 8, 128], F32, S0)
            r_condb = Res()
            cp("dve", condb[:], cond[:].unsqueeze(2).to_broadcast([128, 8, 128]), [r_cond], [r_condb])
            bada_g, r_bada_g = load("bada_g", bada_g_d, [128, 2, 1024], F32, S0)
            npost_mix, r_npost_mix = load("npost_mix", npost_mix_d, [128, 1024], F32, S0)
            npost_ffn, r_npost_ffn = load("npost_ffn", npost_ffn_d, [128, 1024], F32, S0)
            waR = Ring(nc, S0, "wa", [128, 8, 512], F32, 2)
            pcol, r_pcol = banks[0], bres[0]
            for q in range(12):
                wa, r_wa = waR.get()
                P.dma("sp", wa[:], w_ada[:, q * 512:(q + 1) * 512].rearrange("(c p) n -> p c n", p=128), writes=[r_wa])
                if q in (4, 5, 10, 11):
                    gi = 0 if q < 6 else 1
                    half = q % 2
                    bk, rbk = bank(1, 8)
                    for kc in range(8):
                        mm(bk[:, :], condb[:, kc, :], wa[:, kc, :], kc == 0, kc == 7, [r_condb, r_wa], [rbk])
                    dst = (g1n if gi == 0 else g2n)
                    rd = (r_g1n if gi == 0 else r_g2n)
                    tt("dve", dst[:, half * 512:(half + 1) * 512], bk[:, :], bada_g[:, gi, half * 512:(half + 1) * 512],
                       ALU.add, [rbk, r_bada_g], [rd])
                else:
                    for jj in range(4):
                        j = q * 4 + jj
                        for kc in range(8):
                            mm(pcol[:, j:j + 1], wa[:, kc, jj * 128:(jj + 1) * 128], cond[:, kc:kc + 1], kc == 0, kc == 7,
                               [r_cond, r_wa], [r_pcol])
            ms("dve", modcol[:], 0.0, [r_modcol])
            tt("dve", modcol[:, 0:16], pcol[:, 0:16], bada_col[:, 0:16], ALU.add, [r_pcol, r_bada_col], [r_modcol])
            tt("dve", modcol[:, 24:40], pcol[:, 24:40], bada_col[:, 24:40], ALU.add, [r_pcol, r_bada_col], [r_modcol])
            tt("dve", g1n[:], g1n[:], npost_mix[:], ALU.mult, [r_g1n, r_npost_mix], [r_g1n])
            tt("dve", g2n[:], g2n[:], npost_ffn[:], ALU.mult, [r_g2n, r_npost_ffn], [r_g2n])
            stt("dve", gsc1[:], modcol[:, 8:16], 1.0, npre_mix[:], ALU.add, ALU.mult, [r_modcol, r_npre_mix], [r_gsc])
            stt("dve", gsc2[:], modcol[:, 32:40], 1.0, npre_ffn[:], ALU.add, ALU.mult, [r_modcol, r_npre_ffn, r_gsc], [r_gsc])
            P.barrier()
            ck("p0", modcol=modcol[:], g1n=g1n[:], g2n=g2n[:], gsc1=gsc1[:], gsc2=gsc2[:], neglam=neglam[:])
        sh1 = modcol[:, 0:8]
        sh2 = modcol[:, 24:32]

        def norm_transpose(xt, r_xt, gsc, sh, dstT, tok0, r_dst, sqj, r_sqj, stat, r_stat, xnR, defer=None):
            act(sqj[:], xt[:], AF.Square, [r_xt], [r_sqj, r_stat], accum=stat[:, 0:1])
            act(stat[:, 1:2], stat[:, 0:1], AF.Ln, [r_stat], [r_stat], bias=EPS, scale=1.0 / 1024)
            act(stat[:, 2:3], stat[:, 1:2], AF.Exp, [r_stat], [r_stat], scale=-0.5)
            xn, r_xn = xnR.get()
            tsc("dve", xn[:], xt[:], stat[:, 2:3], None, ALU.mult, None, [r_xt, r_stat], [r_xn])

            def part2():
                bk, rbk = bank()
                bv = bk[:].bitcast(BF16)
                for c in range(8):
                    tr(bv[:, c * 128:(c + 1) * 128], xn[:, c * 128:(c + 1) * 128], identb[:], [r_xn, r_identb], [rbk])
                for c in range(8):
                    o = dstT[:, c, tok0:tok0 + 128]
                    i = bv[:, c * 128:(c + 1) * 128]
                    if c % 2 == 0:
                        tsc("dve", o, i, gsc[:, c:c + 1], sh[:, c:c + 1], ALU.mult, ALU.add, [rbk, r_gsc, r_modcol], [r_dst])
                    else:
                        act(o, i, AF.Identity, [rbk, r_gsc, r_modcol], [r_dst], bias=sh[:, c:c + 1], scale=gsc[:, c:c + 1])

            if defer is None:
                part2()
            else:
                defer.append(part2)

        for pas in range(2):
            own = pas == 1
            T0 = pas * 2048
            with contextlib.ExitStack() as SP:
                hT = sb("hT", [128, 8, 2048], BF16, SP)
                r_hT = [Res() for _ in range(16)]

                def hres(tok0, ntok):
                    return r_hT[tok0 // 128:(tok0 + ntok + 127) // 128]

                wst = Ring(nc, SP, "wst", [128, 8, 128], F32, 2)
                wbf = Ring(nc, SP, "wbf", [128, 8, 128], BF16, 3)

                def load_wblk(col0, ncols):
                    s_t, s_r = wst.get()
                    b_t, b_r = wbf.get()
                    P.dma("sp", s_t[:, :, 0:ncols], w_in[:, col0:col0 + ncols].rearrange("(c p) n -> p c n", p=128),
                          writes=[s_r])
                    act(b_t[:, :, 0:ncols], s_t[:, :, 0:ncols], AF.Copy, [s_r], [b_r])
                    return b_t, b_r

                def proj_fm(wb, r_wb, tok0, ntok, bk, rbk):
                    for kc in range(8):
                        mm(bk[:, 0:ntok], wb[:, kc, :], hT[:, kc, tok0:tok0 + ntok], kc == 0, kc == 7,
                           [r_wb] + hres(tok0, ntok), [rbk])

                with contextlib.ExitStack() as SA:
                    xR = Ring(nc, SA, "xt", [128, 1024], F32, 3)
                    xnR = Ring(nc, SA, "xn", [128, 1024], BF16, 2)
                    sqj = sb("sqj", [128, 1024], BF16, SA)
                    r_sqj = Res()
                    statR = Ring(nc, SA, "stat", [128, 4], F32, 2)
                    prev_p = []
                    for t in range(16):
                        xt, r_xt = xR.get()
                        P.dma("sp", xt[:], xin[T0 + t * 128:T0 + (t + 1) * 128, :], writes=[r_xt])
                        stat, r_stat = statR.get()
                        cur_p = []
                        norm_transpose(xt, r_xt, gsc1, sh1, hT, t * 128, r_hT[t], sqj, r_sqj, stat, r_stat, xnR, defer=cur_p)
                        while prev_p:
                            prev_p.pop(0)()
                        prev_p = cur_p
                    while prev_p:
                        prev_p.pop(0)()
                    P.barrier()
                    ck("A%d" % pas, hT=hT[:])

                def run_ssd(SS, inject):
                    xs_tm = sb("xs_tm", [128, 16, 512], BF16, SS)
                    r_xs = [Res() for _ in range(16)]
                    Btm = sb("Btm", [128, 16, 128], BF16, SS)
                    r_Btm = [Res() for _ in range(16)]
                    uR = Ring(nc, SS, "u", [128, 3 + 1024], F32, 2)
                    accR = Ring(nc, SS, "acc", [128, 1024], F32, 1)
                    xcR = Ring(nc, SS, "xc", [128, 1024], BF16, 2)
                    dtg = sb("dtg", [128, 16, 8], F32, SS)
                    ag = sb("ag", [128, 16, 8], F32, SS)
                    acsg = sb("acsg", [128, 16, 8], F32, SS)
                    dteg = sb("dteg", [128, 16, 8], F32, SS)
                    ecsg = sb("ecsg", [128, 16, 8], F32, SS)
                    cdg = sb("cdg", [128, 16, 8], F32, SS)
                    wdteg = sb("wdteg", [128, 16, 8], F32, SS)
                    tmp8 = sb("tmp8", [128, 16, 8], F32, SS)
                    r_dec = Res()
                    prev = sb("prev", [128, 512], F32, SS)
                    r_prev = Res()
                    ptmpR = Ring(nc, SS, "ptmp", [128, 512], F32, 1)
                    xdteR = Ring(nc, SS, "xdte", [128, 512], BF16, 4)
                    if own:
                        BT = sb("BT", [128, 2048], BF16, SS)
                        CT = sb("CT", [128, 2048], BF16, SS)
                        r_BT = [Res() for _ in range(2)]
                        r_CT = [Res() for _ in range(2)]
                        zs = sb("zs", [128, 16, 512], BF16, SS)
                        r_zs = [Res() for _ in range(16)]
                        ynT = sb("ynT", [128, 4, 2048], BF16, SS)
                        r_ynT = Res()
                        sel, r_sel = load("sel", sel_d, [8, 8, 128], F32, SS)
                        xdtR = Ring(nc, SS, "xdt", [128, 512], BF16, 3)
                        pbfR = Ring(nc, SS, "pbf", [128, 512], BF16, 2)
                        scTR = Ring(nc, SS, "scT", [128, 128], F32, 4)
                        acsTR = Ring(nc, SS, "acsT", [8, 128], F32, 4)
                        ahR = Ring(nc, SS, "ah", [8, 2, 128], BF16, 4)
                        bdR = Ring(nc, SS, "bd", [8, 2, 512], BF16, 3)
                        negm4b = sb("negm4b", [128, 512], BF16, SS)
                        r_negm4b = Res()
                        cp("pool", negm4b[:].rearrange("p (h l) -> p h l", h=4), negmb[:].unsqueeze(1).to_broadcast([128, 4, 128]), [r_negmb], [r_negm4b])
                        selb = sb("selb", [8, 8, 128], BF16, SS)
                        nselb = sb("nselb", [8, 8, 128], BF16, SS)
                        onesb = sb("onesb", [8, 128], BF16, SS)
                        r_selb = Res()
                        r_nselb = Res()
                        r_onesb = Res()
                        cp("pool", selb[:], sel[:], [r_sel], [r_selb])
                        tsc("dve", nselb[:], sel[:], -1.0, None, ALU.mult, None, [r_sel], [r_nselb])
                        ms("pool", onesb[:], 1.0, [r_onesb])
                        decR = Ring(nc, SS, "dec", [128, 512], F32, 2)
                        MTR = Ring(nc, SS, "MT", [128, 8, 128], BF16, 3)
                        y1R = Ring(nc, SS, "y1", [128, 512], F32, 1)
                        y2R = Ring(nc, SS, "y2", [128, 512], F32, 1)
                        y3R = Ring(nc, SS, "y3", [128, 512], F32, 2)
                        ynR = Ring(nc, SS, "yn", [128, 512], BF16, 2)
                        sqy = sb("sqy", [128, 512], BF16, SS)
                        r_sqy = Res()
                        ystR = Ring(nc, SS, "yst", [128, 4], F32, 2)

                    pending_tail = []

                    def flush_tail():
                        while pending_tail:
                            pending_tail.pop(0)()

                    def conv_block(blk, col0, kind, first_tok=0):
                        wb, r_wb = load_wblk(col0, 128)
                        u_prev = None
                        for q in range(first_tok // 1024, 2):
                            u, r_u = uR.get()
                            if q == first_tok // 1024:
                                if own:
                                    cp("pool", u[:, 0:3], utail[:, blk, :], [r_utail[blk]], [r_u])
                                else:
                                    ms("pool", u[:, 0:3], 0.0, [r_u])
                            else:
                                cp("pool", u[:, 0:3], u_prev[0][:, 1024:1027], [u_prev[1]], [r_u])
                            g0 = 0
                            if first_tok > q * 1024:
                                g0 = (first_tok - q * 1024) // 512
                                ms("pool", u[:, 3:3 + g0 * 512], 0.0, [r_u])
                            for gq in range(g0, 2):
                                bk, rbk = bank()
                                proj_fm(wb, r_wb, q * 1024 + gq * 512, 512, bk, rbk)
                                if gq == 1:
                                    flush_tail()
                                if own:
                                    act(u[:, 3 + gq * 512:3 + (gq + 1) * 512], bk[:, :], AF.Copy, [rbk], [r_u])
                                else:
                                    act(u[:, 3 + gq * 512:3 + (gq + 1) * 512], bk[:, :], AF.Copy, [rbk, r_valid], [r_u],
                                        scale=valid[:, 0:1])
                            u_prev = (u, r_u)
                            if (not own) and q == 1:
                                cp("pool", utail[:, blk, :], u[:, 1024:1027], [r_u], [r_utail[blk]])
                            if kind == "C" and not own:
                                continue
                            acc, r_acc = accR.get()
                            tsc("dve", acc[:], u[:, 3:1027], convw[:, blk, 3:4], convb[:, blk:blk + 1], ALU.mult, ALU.add,
                                [r_u, r_convw, r_convb], [r_acc])
                            for k in (2, 1, 0):
                                stt("dve", acc[:], u[:, k:k + 1024], convw[:, blk, k:k + 1], acc[:], ALU.mult, ALU.add,
                                    [r_u, r_convw, r_acc], [r_acc])
                            xc, r_xc = xcR.get()
                            act(xc[:], acc[:], AF.Silu, [r_acc], [r_xc])
                            if inject is not None:
                                inject()
                            if kind == "C":
                                cp("pool", CT[:, q * 1024:(q + 1) * 1024], xc[:], [r_xc], [r_CT[q]])
                                continue
                            if kind == "B" and own:
                                cp("pool", BT[:, q * 1024:(q + 1) * 1024], xc[:], [r_xc], [r_BT[q]])
                            def tail(xc=xc, r_xc=r_xc, q=q, kind=kind):
                                bk, rbk = bank()
                                bv = bk[:].bitcast(BF16)
                                for i in range(8):
                                    tr(bv[:, i * 128:(i + 1) * 128], xc[:, i * 128:(i + 1) * 128], identb[:], [r_xc, r_identb], [rbk])
                                if kind == "B":
                                    cp("dve", Btm[:, q * 8:(q + 1) * 8, :], bv.rearrange("p (t c) -> p t c", t=8), [rbk],
                                       r_Btm[q * 8:(q + 1) * 8])
                                else:
                                    j = kind
                                    act(xs_tm[:, q * 8:(q + 1) * 8, j * 128:(j + 1) * 128], bv.rearrange("p (t c) -> p t c", t=8),
                                        AF.Copy, [rbk], r_xs[q * 8:(q + 1) * 8])
                            pending_tail.append(tail)

                    for g in range(4):
                        wdt, r_wdt = load_wblk(5120 + 8 * g, 8)
                        bk, rbk = bank()
                        for c in range(16):
                            for kc in range(8):
                                mm(bk[:, c * 8:(c + 1) * 8], hT[:, kc, c * 128:(c + 1) * 128], wdt[:, kc, 0:8], kc == 0, kc == 7,
                                   [r_wdt, r_hT[c]], [rbk])
                        bkv = bk[:, 0:128].rearrange("p (c h) -> p c h", h=8)
                        tt("dve", tmp8[:], bkv, dtb[:, 8 * g:8 * g + 8].unsqueeze(1).to_broadcast([128, 16, 8]), ALU.add,
                           [rbk, r_dtb], [r_dec])
                        act(tmp8[:], tmp8[:], AF.Exp, [r_dec], [r_dec])
                        act(dtg[:], tmp8[:], AF.Ln, [r_dec], [r_dec], bias=1.0)
                        tt("dve", ag[:], dtg[:], Arep[:, 8 * g:8 * g + 8].unsqueeze(1).to_broadcast([128, 16, 8]), ALU.mult,
                           [r_dec, r_Arep], [r_dec])
                        bkc, rbkc = bank()
                        bkt, rbkt = bank()
                        for c in range(16):
                            mm(bkc[:, c * 8:(c + 1) * 8], utri[:], ag[:, c, :], True, True, [r_utri, r_dec], [rbkc])
                            mm(bkt[:, c * 8:(c + 1) * 8], onesf[:], ag[:, c, :], True, True, [r_onesf, r_dec], [rbkt])
                        bkcv = bkc[:, 0:128].rearrange("p (c h) -> p c h", h=8)
                        bktv = bkt[:, 0:128].rearrange("p (c h) -> p c h", h=8)
                        cp("dve", acsg[:], bkcv, [rbkc], [r_dec])
                        act(ecsg[:], acsg[:], AF.Exp, [r_dec], [r_dec])
                        act(cdg[:], bktv, AF.Exp, [rbkt, r_dec], [r_dec])
                        tt("dve", tmp8[:], bktv, acsg[:], ALU.subtract, [rbkt, r_dec], [r_dec])
                        act(dteg[:], tmp8[:], AF.Exp, [r_dec], [r_dec])
                        tt("dve", wdteg[:], dteg[:], dtg[:], ALU.mult, [r_dec], [r_dec])

                        for j in range(4):
                            conv_block(4 * g + j, 2048 + (4 * g + j) * 128, j)
                        conv_block(16 + g, 2048 + 2048 + g * 128, "B")
                        if own:
                            conv_block(20 + g, 2048 + 2560 + g * 128, "C")
                        else:
                            conv_block(20 + g, 2048 + 2560 + g * 128, "C", first_tok=1536)
                        flush_tail()
                        if own:
                            for j in range(4):
                                wz, r_wz = load_wblk((4 * g + j) * 128, 128)
                                for t in range(16):
                                    if t % 4 == 0:
                                        bk, rbk = bank()
                                    for kc in range(8):
                                        mm(bk[:, (t % 4) * 128:(t % 4 + 1) * 128], hT[:, kc, t * 128:(t + 1) * 128], wz[:, kc, :],
                                           kc == 0, kc == 7, [r_wz, r_hT[t]], [rbk])
                                    if t % 4 == 3:
                                        act(zs[:, t - 3:t + 1, j * 128:(j + 1) * 128], bk[:, :].rearrange("p (t c) -> p t c", t=4),
                                            AF.Silu, [rbk], r_zs[t - 3:t + 1])

                        if own:
                            cp("dve", prev[:], state0[:, g, :], [r_state0[g]], [r_prev])
                        else:
                            ms("dve", prev[:], 0.0, [r_prev])
                        st1 = {}

                        def xsv_(c):
                            return xs_tm[:, c, :].rearrange("p (h d) -> p h d", h=8)

                        def S1(c):
                            d = {}
                            xdte, r_xdte = xdteR.get()
                            tt("pool", xdte[:].rearrange("p (h d) -> p h d", h=8), xsv_(c),
                               wdteg[:, c, :].unsqueeze(2).to_broadcast([128, 8, 64]), ALU.mult, [r_xs[c], r_dec], [r_xdte])
                            d["xdte"] = (xdte, r_xdte)
                            if own:
                                xdt, r_xdt = xdtR.get()
                                tt("pool", xdt[:].rearrange("p (h d) -> p h d", h=8), xsv_(c),
                                   dtg[:, c, :].unsqueeze(2).to_broadcast([128, 8, 64]), ALU.mult, [r_xs[c], r_dec], [r_xdt])
                                d["xdt"] = (xdt, r_xdt)
                                cs = slice(c * 128, (c + 1) * 128)
                                rB = r_BT[c // 8]
                                rC = r_CT[c // 8]
                                bks, rbks = bank()
                                mm(bks[:, 0:128], BT[:, cs], CT[:, cs], True, True, [rB, rC], [rbks])
                                scT, r_scT = scTR.get()
                                act(scT[:], bks[:, 0:128], AF.Copy, [rbks], [r_scT])
                                bka, rbka = bank()
                                mm(bka[0:8, 0:128], ag[:, c, :], utri[:], True, True, [r_dec, r_utri], [rbka])
                                acsT, r_acsT = acsTR.get()
                                ah, r_ah = ahR.get()
                                act(acsT[:], bka[0:8, 0:128], AF.Copy, [rbka], [r_acsT])
                                act(ah[:, 0, :], bka[0:8, 0:128], AF.Copy, [rbka], [r_ah])
                                tt("pool", ah[:, 1, :], acsT[:], ah[:, 0, :], ALU.subtract, [r_acsT, r_ah], [r_ah])
                                d["ah"] = (ah, r_ah)
                                d["scT"] = (scT, r_scT)
                            st1[c] = d

                        def S1r(c):
                            d = st1[c]
                            if True:
                                ah, r_ah = d["ah"]
                                MT, r_MT = MTR.get()
                                decs = []
                                for hh in range(2):
                                    bkr, rbkr = bank()
                                    nsv = nselb[:, hh * 4:(hh + 1) * 4, :].rearrange("k h l -> k (h l)")
                                    first = True
                                    for h4 in range(4):
                                        for t2_ in range(2):
                                            mm(bkr[:, h4 * 128:(h4 + 1) * 128], selb[:, hh * 4 + h4, :], ah[:, t2_, :], first, False,
                                               [r_selb, r_ah], [rbkr], sgc=True)
                                            first = False
                                    mm(bkr[:, :], ah[:, 0, :], nsv, False, False, [r_nselb, r_ah], [rbkr], sgc=True)
                                    mm(bkr[:, :], ah[:, 1, :], nsv, False, False, [r_nselb, r_ah], [rbkr], sgc=True)
                                    mm(bkr[:, :], identb[:], negm4b[:], False, True, [r_identb, r_negm4b], [rbkr], sgc=True)
                                    dec, r_dc = decR.get()
                                    act(dec[:], bkr[:, :], AF.Exp, [rbkr], [r_dc])
                                    decs.append((dec, r_dc))
                                d["MT"] = (MT, r_MT)
                                d["decs"] = decs

                        def S1b(c):
                            d = st1[c]
                            MT, r_MT = d["MT"]
                            scT, r_scT = d["scT"]
                            for hh in range(2):
                                dec, r_dc = d["decs"][hh]
                                tt("dve", MT[:, hh * 4:(hh + 1) * 4, :], dec[:].rearrange("p (h l) -> p h l", h=4),
                                   scT[:].unsqueeze(1).to_broadcast([128, 4, 128]), ALU.mult, [r_dc, r_scT], [r_MT])

                        def S2a(c, pbf, r_pbf):
                            d = st1[c]
                            MT, r_MT = d["MT"]
                            xdt, r_xdt = d["xdt"]
                            cs = slice(c * 128, (c + 1) * 128)
                            rC = r_CT[c // 8]
                            bky, rbky = bank()
                            for h in range(8):
                                mm(bky[:, h * 64:(h + 1) * 64], MT[:, h, :], xdt[:, h * 64:(h + 1) * 64], True, True,
                                   [r_MT, r_xdt], [rbky])
                            bko, rbko = bank()
                            mm(bko[:, :], CT[:, cs], pbf[:], True, True, [rC, r_pbf], [rbko])
                            y1, r_y1 = y1R.get()
                            tt("dve", y1[:].rearrange("p (h d) -> p h d", h=8), bko[:, :].rearrange("p (h d) -> p h d", h=8),
                               ecsg[:, c, :].unsqueeze(2).to_broadcast([128, 8, 64]), ALU.mult, [rbko, r_dec], [r_y1])
                            y2, r_y2 = y2R.get()
                            tt("dve", y2[:], bky[:, :], y1[:], ALU.add, [rbky, r_y1], [r_y2])
                            y3, r_y3 = y3R.get()
                            tt("pool", y3[:].rearrange("p (h d) -> p h d", h=8), xsv_(c),
                               dskip[:, 8 * g:8 * g + 8].unsqueeze(2).to_broadcast([128, 8, 64]), ALU.mult,
                               [r_xs[c], r_dskip], [r_y3])
                            tt("dve", y3[:], y3[:], y2[:], ALU.add, [r_y3, r_y2], [r_y3])
                            tt("dve", y3[:], y3[:], zs[:, c, :], ALU.mult, [r_y3, r_zs[c]], [r_y3])
                            yst, r_yst = ystR.get()
                            act(sqy[:], y3[:], AF.Square, [r_y3], [r_sqy, r_yst], accum=yst[:, 0:1])
                            act(yst[:, 1:2], yst[:, 0:1], AF.Ln, [r_yst], [r_yst], bias=EPS, scale=1.0 / 512)
                            act(yst[:, 2:3], yst[:, 1:2], AF.Exp, [r_yst], [r_yst], scale=-0.5)
                            yn, r_yn = ynR.get()
                            tsc("dve", yn[:], y3[:], yst[:, 2:3], None, ALU.mult, None, [r_y3, r_yst], [r_yn])
                            d["yn"] = (yn, r_yn)

                        def S2b(c):
                            yn, r_yn = st1[c]["yn"]
                            cs = slice(c * 128, (c + 1) * 128)
                            bkt2, rbkt2 = bank()
                            bv = bkt2[:].bitcast(BF16)
                            for j in range(4):
                                tr(bv[:, j * 128:(j + 1) * 128], yn[:, j * 128:(j + 1) * 128], identb[:], [r_yn, r_identb], [rbkt2])
                            act(ynT[:, :, cs], bv[:, 0:512].rearrange("p (j t) -> p j t", j=4), AF.Copy, [rbkt2], [r_ynT])
                            del st1[c]

                        def S3(c):
                            xdte, r_xdte = st1[c]["xdte"]
                            bkS, rbkS = bank()
                            mm(bkS[:, :], Btm[:, c, :], xdte[:], True, True, [r_Btm[c], r_xdte], [rbkS])
                            ptmp, r_ptmp = ptmpR.get()
                            tt("dve", ptmp[:].rearrange("p (h d) -> p h d", h=8), prev[:].rearrange("p (h d) -> p h d", h=8),
                               cdg[:, c, :].unsqueeze(2).to_broadcast([128, 8, 64]), ALU.mult, [r_prev, r_dec], [r_ptmp])
                            tt("dve", prev[:], bkS[:, :], ptmp[:], ALU.add, [rbkS, r_ptmp], [r_prev])

                        def mkpbf():
                            pbf, r_pbf = pbfR.get()
                            act(pbf[:], prev[:], AF.Copy, [r_prev], [r_pbf])
                            return pbf, r_pbf

                        if own:
                            for c0 in range(2):
                                S1(c0)
                                S1r(c0)
                                S1b(c0)
                            pb = mkpbf()
                            for c in range(16):
                                if c + 2 < 16:
                                    S1(c + 2)
                                pb_next = None
                                if c < 15:
                                    S3(c)
                                    pb_next = mkpbf()
                                S2a(c, *pb)
                                if c >= 1:
                                    S2b(c - 1)
                                if c + 2 < 16:
                                    S1r(c + 2)
                                    S1b(c + 2)
                                pb = pb_next
                            S2b(15)
                        else:
                            S1(0)
                            S1(1)
                            for c in range(16):
                                if c + 2 < 16:
                                    S1(c + 2)
                                S3(c)
                                del st1[c]
                                if inject is not None:
                                    inject()
                        if own:
                            for j in range(4):
                                P.dma("sp", ynT_d[g * 4 + j], ynT[:, j, :], reads=[r_ynT])
                        else:
                            tsc("dve", state0[:, g, :], prev[:], valid[:, 0:1], None, ALU.mult, None, [r_prev, r_valid],
                                [r_state0[g]])
                        ck("ssd%d_g%d" % (pas, g), state0=state0[:], xs_tm=xs_tm[:], Btm=Btm[:], dtg=dtg[:], dteg=dteg[:], cdg=cdg[:])
                    P.barrier()
                    ck("ssd%d" % pas, state0=state0[:], utail=utail[:])

                def run_att(SAt, ssd_call):
                    cosT = sb("cosT", [128, 2048], F32, SAt)
                    sinT = sb("sinT", [128, 2048], F32, SAt)
                    r_cos = Res()
                    r_sin = Res()
                    with contextlib.ExitStack() as SR:
                        pi_, r_pi = load("posi", pos_d[:, T0:T0 + 2048], [128, 2048], I32, SR)
                        ang = sb("ang", [128, 2048], F32, SR)
                        kf = sb("kf", [128, 2048], F32, SR)
                        ki = sb("ki", [128, 2048], I32, SR)
                        a2 = sb("a2", [128, 2048], F32, SR)
                        r_ang = Res()
                        r_kf = Res()
                        r_ki = Res()
                        r_a2 = Res()
                        cp("dve", ang[:], pi_[:], [r_pi], [r_ang])
                        tsc("dve", ang[:], ang[:], invf[:, 0:1], None, ALU.mult, None, [r_ang, r_invf], [r_ang])
                        for (dst, r_dst, shift) in ((sinT, r_sin, 0.0), (cosT, r_cos, math.pi / 2)):
                            tsc("dve", a2[:], ang[:], shift, None, ALU.add, None, [r_ang], [r_a2])
                            tsc("dve", kf[:], a2[:], 1.0 / (2 * math.pi), None, ALU.mult, None, [r_a2], [r_kf])
                            cp("dve", ki[:], kf[:], [r_kf], [r_ki])
                            cp("dve", kf[:], ki[:], [r_ki], [r_kf])
                            stt("dve", a2[:], kf[:], -2 * math.pi, a2[:], ALU.mult, ALU.add, [r_kf, r_a2], [r_a2])
                            tsc("dve", kf[:], a2[:], math.pi, -2 * math.pi, ALU.is_gt, ALU.mult, [r_a2], [r_kf])
                            tt("dve", a2[:], a2[:], kf[:], ALU.add, [r_a2, r_kf], [r_a2])
                            tsc("dve", kf[:], a2[:], -math.pi, 2 * math.pi, ALU.is_lt, ALU.mult, [r_a2], [r_kf])
                            tt("dve", a2[:], a2[:], kf[:], ALU.add, [r_a2, r_kf], [r_a2])
                            act(dst[:], a2[:], AF.Sin, [r_a2], [r_dst])
                        P.barrier()
                        ck("rope%d" % pas, cosT=cosT[:], sinT=sinT[:])


                    kT = sb("kT", [128, 4096], BF16, SAt)
                    r_kTpre = Res()
                    r_kTown = [Res() for _ in range(4)]
                    Vaug = sb("Vaug", [128, 32, 130], BF16, SAt)
                    r_Vpre = Res()
                    r_Vown = [Res() for _ in range(16)]
                    qT = sb("qT", [128, 2048], BF16, SAt)
                    r_qT = [Res() for _ in range(4)]
                    rawR = Ring(nc, SAt, "raw", [128, 512], BF16, 2)
                    t1R = Ring(nc, SAt, "t1", [128, 512], F32, 2)
                    t2R = Ring(nc, SAt, "t2", [128, 512], F32, 2)
                    if own:
                        ER = Ring(nc, SAt, "E", [128, 256], BF16, 10)
                        o1R = Ring(nc, SAt, "o1", [128, 128], F32, 2)
                        acS0R = Ring(nc, SAt, "acS0", [128, 2, 132], F32, 2)
                        acS1R = Ring(nc, SAt, "acS1", [128, 2, 132], F32, 2)
                        o2R = Ring(nc, SAt, "o2", [128, 128], F32, 2)
                        onR = Ring(nc, SAt, "on", [128, 128], BF16, 2)
                        astR = Ring(nc, SAt, "ast", [128, 8], F32, 2)
                        oTR = Ring(nc, SAt, "oTst", [128, 2048], BF16, 2)
                    sqo = sb("sqo", [128, 128], BF16, SAt)
                    r_sqo = Res()
                    koff = 2048 if own else 0

                    def rope_block(col0, dst, dst_off, r_dst_list):
                        wb, r_wb = load_wblk(col0, 128)
                        pend = []

                        def stage2(gq, bkA, rbkA, raw, r_raw):
                            bkB, rbkB = bank()
                            mm(bkB[:, :], protb[:], raw[:], True, True, [r_protb, r_raw], [rbkB])
                            t1, r_t1 = t1R.get()
                            t2, r_t2 = t2R.get()
                            ts = slice(gq * 512, (gq + 1) * 512)
                            tt("dve", t1[:], bkA[:, :], cosT[:, ts], ALU.mult, [rbkA, r_cos], [r_t1])
                            tt("dve", t2[:], bkB[:, :], sinT[:, ts], ALU.mult, [rbkB, r_sin], [r_t2])
                            tt("pool", dst[:, dst_off + gq * 512:dst_off + (gq + 1) * 512], t1[:], t2[:], ALU.add, [r_t1, r_t2],
                               [r_dst_list[gq]])

                        for gq in range(4):
                            bkA, rbkA = bank()
                            proj_fm(wb, r_wb, gq * 512, 512, bkA, rbkA)
                            raw, r_raw = rawR.get()
                            act(raw[:], bkA[:, :], AF.Copy, [rbkA], [r_raw])
                            if pend:
                                stage2(*pend.pop(0))
                            pend.append((gq, bkA, rbkA, raw, r_raw))
                        stage2(*pend.pop(0))

                    nset = 2 if own else 1
                    bufs = [dict(kT=kT, r_kTpre=r_kTpre, r_kTown=r_kTown, Vaug=Vaug, r_Vpre=r_Vpre, r_Vown=r_Vown, qT=qT, r_qT=r_qT)]
                    if own:
                        bufs.append(dict(kT=sb("kT2", [128, 4096], BF16, SAt), r_kTpre=Res(), r_kTown=[Res() for _ in range(4)],
                                         Vaug=sb("Vaug2", [128, 32, 130], BF16, SAt), r_Vpre=Res(), r_Vown=[Res() for _ in range(16)],
                                         qT=sb("qT2", [128, 2048], BF16, SAt), r_qT=[Res() for _ in range(4)]))

                    def rope_block_gen(col0, dst, dst_off, r_dst_list):
                        wb, r_wb = load_wblk(col0, 128)
                        pend = []

                        def stage2(gq, bkA, rbkA, raw, r_raw):
                            bkB, rbkB = bank(0, 6)
                            mm(bkB[:, :], protb[:], raw[:], True, True, [r_protb, r_raw], [rbkB])
                            t1, r_t1 = t1R.get()
                            t2, r_t2 = t2R.get()
                            ts = slice(gq * 512, (gq + 1) * 512)
                            tt("dve", t1[:], bkA[:, :], cosT[:, ts], ALU.mult, [rbkA, r_cos], [r_t1])
                            tt("dve", t2[:], bkB[:, :], sinT[:, ts], ALU.mult, [rbkB, r_sin], [r_t2])
                            tt("pool", dst[:, dst_off + gq * 512:dst_off + (gq + 1) * 512], t1[:], t2[:], ALU.add, [r_t1, r_t2],
                               [r_dst_list[gq]])

                        for gq in range(4):
                            bkA, rbkA = bank(0, 6)
                            proj_fm(wb, r_wb, gq * 512, 512, bkA, rbkA)
                            raw, r_raw = rawR.get()
                            act(raw[:], bkA[:, :], AF.Copy, [rbkA], [r_raw])
                            if pend:
                                stage2(*pend.pop(0))
                            pend.append((gq, bkA, rbkA, raw, r_raw))
                            if gq == 1 and own:
                                stage2(*pend.pop(0))
                                yield
                        stage2(*pend.pop(0))
                        yield

                    def prep_gen(hd, B):
                        kT, r_kTpre, r_kTown = B["kT"], B["r_kTpre"], B["r_kTown"]
                        Vaug, r_Vpre, r_Vown = B["Vaug"], B["r_Vpre"], B["r_Vown"]
                        qT, r_qT = B["qT"], B["r_qT"]
                        if own:
                            P.dma("sp", kT[:, 0:2048], kpre_d[hd], writes=[r_kTpre])
                            P.dma("sp", Vaug[:, 0:16, :].rearrange("p t c -> p (t c)"), vpre_d[hd], writes=[r_Vpre])
                        yield from rope_block_gen(6176 + hd * 128, kT, koff, r_kTown)
                        wv, r_wv = load_wblk(7200 + hd * 128, 128)
                        vo = 16 if own else 0
                        for t in range(16):
                            if t % 4 == 0:
                                bk, rbk = bank(0, 6)
                            for kc in range(8):
                                mm(bk[:, (t % 4) * 128:(t % 4 + 1) * 128], hT[:, kc, t * 128:(t + 1) * 128], wv[:, kc, :],
                                   kc == 0, kc == 7, [r_wv, r_hT[t]], [rbk])
                            if t % 4 == 3:
                                src = bk[:, :].rearrange("p (t c) -> p t c", t=4)
                                if own:
                                    act(Vaug[:, vo + t - 3:vo + t + 1, 0:128], src, AF.Copy, [rbk], r_Vown[t - 3:t + 1])
                                else:
                                    act(Vaug[:, t - 3:t + 1, 0:128], src, AF.Copy, [rbk, r_valid], r_Vown[t - 3:t + 1],
                                        scale=valid[:, 0:1])
                            if t == 7 and own:
                                yield
                        if own:
                            ms("pool", Vaug[:, 16:32, 128:130], 1.0, r_Vown)
                        else:
                            cp("pool", Vaug[:, 0:16, 128:130], valid[:, 0:1].unsqueeze(1).to_broadcast([128, 16, 2]), [r_valid],
                               r_Vown)
                        if not own:
                            P.dma("sp", kpre_d[hd], kT[:, 0:2048], reads=r_kTown)
                            P.dma("sp", vpre_d[hd], Vaug[:, 0:16, :].rearrange("p t c -> p (t c)"), reads=r_Vown)
                            return
                        yield
                        yield from rope_block_gen(5152 + hd * 128, qT, 0, r_qT)

                    def sweep(hd, B, inject):
                        kT, r_kTpre, r_kTown = B["kT"], B["r_kTpre"], B["r_kTown"]
                        Vaug, r_Vpre, r_Vown = B["Vaug"], B["r_Vpre"], B["r_Vown"]
                        qT, r_qT = B["qT"], B["r_qT"]
                        oTs, r_oTs = oTR.get()
                        LOOK = 2
                        for QG in range(8):
                            nk = 16 + 2 * QG + 2
                            accb = [(banks[6 + c], bres[6 + c]) for c in range(2)]
                            pend = {}
                            for step in range(nk + LOOK):
                                if step < nk:
                                    kt = step
                                    j = kt - 16
                                    q0 = max(0, j - 2 * QG) if j >= 0 else 0
                                    diag = j >= 2 * QG
                                    rk = [r_kTpre] if j < 0 else [r_kTown[j // 4]]
                                    nq = 256 - q0 * 128
                                    qs = slice(QG * 256 + q0 * 128, QG * 256 + 256)
                                    sb_ = [bank(0, 6) for _ in range(2)]
                                    for c in range(2):
                                        bkS, rbkS = sb_[c]
                                        ps = slice(c * 64, (c + 1) * 64)
                                        mm(bkS[:, 0:nq], kT[ps, kt * 128:(kt + 1) * 128], qT[ps, qs], True, not diag,
                                           rk + [r_qT[QG // 2]], [rbkS])
                                    Es = []
                                    for c in range(2):
                                        bkS, rbkS = sb_[c]
                                        if diag:
                                            mm(bkS[:, 0:128], identb[:], negmb[:], False, True, [r_identb, r_negmb], [rbkS])
                                        E, r_E = ER.get()
                                        act(E[:, 0:nq], bkS[:, 0:nq], AF.Exp, [rbkS], [r_E], scale=0.125)
                                        Es.append((E, r_E))
                                    pend[kt] = (Es, q0)
                                if step >= LOOK:
                                    kt = step - LOOK
                                    Es, q0 = pend.pop(kt)
                                    j = kt - 16
                                    rv = [r_Vpre] if j < 0 else [r_Vown[j]]
                                    for c in range(2):
                                        E, r_E = Es[c]
                                        ab, rab = accb[c]
                                        for qi in range(q0, 2):
                                            last = (j == 2 * QG + qi)
                                            P.op("pe", (lambda e, o=ab[:, qi * 256:qi * 256 + 129], l=E[:, (qi - q0) * 128:(qi - q0 + 1) * 128],
                                                        r=Vaug[:, kt, 0:129], st_=(kt == 0 and qi == 0), sp_=last:
                                                        e.matmul(o, lhsT=l, rhs=r, start=st_, stop=sp_, skip_group_check=True)),
                                                 [r_E] + rv, [rab])
                            as0, r_as0 = acS0R.get()
                            as1, r_as1 = acS1R.get()
                            act(as0[:, :, 0:129], accb[0][0][:, :].rearrange("p (a b) -> p a b", a=2)[:, :, 0:129], AF.Copy,
                                [accb[0][1]], [r_as0])
                            cp("dve", as1[:, :, 0:129], accb[1][0][:, :].rearrange("p (a b) -> p a b", a=2)[:, :, 0:129],
                               [accb[1][1]], [r_as1])
                            for qi in range(2):
                                tq = QG * 2 + qi
                                ra1, ra2 = r_as0, r_as1
                                a1 = as0[:, qi, 0:129]
                                a2 = as1[:, qi, 0:129]
                                ast, r_ast = astR.get()
                                recip(ast[:, 0:1], a1[:, 128:129], [ra1], [r_ast])
                                recip(ast[:, 1:2], a2[:, 128:129], [ra2, r_ast], [r_ast])
                                tt("dve", ast[:, 2:3], ast[:, 1:2], neglam[:], ALU.mult, [r_ast, r_neglam], [r_ast])
                                o1, r_o1 = o1R.get()
                                tsc("dve", o1[:], a1[:, 0:128], ast[:, 0:1], None, ALU.mult, None, [ra1, r_ast], [r_o1])
                                o2, r_o2 = o2R.get()
                                stt("dve", o2[:], a2[:, 0:128], ast[:, 2:3], o1[:], ALU.mult, ALU.add, [ra2, r_ast, r_o1], [r_o2])
                                act(sqo[:], o2[:], AF.Square, [r_o2], [r_sqo, r_ast], accum=ast[:, 3:4])
                                act(ast[:, 4:5], ast[:, 3:4], AF.Ln, [r_ast], [r_ast], bias=EPS, scale=1.0 / 128)
                                act(ast[:, 5:6], ast[:, 4:5], AF.Exp, [r_ast], [r_ast], scale=-0.5)
                                on, r_on = onR.get()
                                tsc("dve", on[:], o2[:], ast[:, 5:6], None, ALU.mult, None, [r_o2, r_ast], [r_on])
                                bkT, rbkT = bank(0, 6)
                                bv = bkT[:].bitcast(BF16)
                                tr(bv[:, 0:128], on[:], identb[:], [r_on, r_identb], [rbkT])
                                act(oTs[:, tq * 128:(tq + 1) * 128], bv[:, 0:128], AF.Copy, [rbkT, r_sublnw], [r_oTs],
                                    scale=sublnw[:, 0:1])
                            if inject is not None:
                                next(inject, None)

                        P.dma("sp", oT_d[hd], oTs[:], reads=[r_oTs])

                    def gate_gen():
                        gstR = Ring(nc, SAt, "gst", [128, 2048], BF16, 2)
                        geR = Ring(nc, SAt, "ge", [128, 512], F32, 2)
                        for blk in range(16):
                            wb, r_wb = load_wblk(8224 + blk * 128, 128)
                            gst, r_gst = gstR.get()
                            for gq in range(4):
                                bk, rbk = bank(0, 6)
                                proj_fm(wb, r_wb, gq * 512, 512, bk, rbk)
                                ge, r_ge = geR.get()
                                act(ge[:], bk[:, :], AF.Exp, [rbk], [r_ge], scale=-1.0)
                                tsc("dve", ge[:], ge[:], 1.0, None, ALU.add, None, [r_ge], [r_ge])
                                recip(ge[:], ge[:], [r_ge], [r_ge])
                                cp("dve", gst[:, gq * 512:(gq + 1) * 512], ge[:], [r_ge], [r_gst])
                                if gq == 1:
                                    yield
                            P.dma("sp", gT_d[blk], gst[:], reads=[r_gst])
                            yield

                    if not own:
                        def allprep():
                            for hd in range(8):
                                yield from prep_gen(hd, bufs[0])
                        pg = allprep()
                        cnt_ = [0]

                        def inj():
                            cnt_[0] += 1
                            if cnt_[0] % 6 == 0:
                                next(pg, None)
                        if ssd_call is not None:
                            ssd_call(inj)
                        for _ in pg:
                            pass
                    else:
                        gg = gate_gen()

                        def both(g1_):
                            while True:
                                if g1_ is not None:
                                    next(g1_, None)
                                next(gg, None)
                                yield

                        for _ in prep_gen(0, bufs[0]):
                            pass
                        for hd in range(8):
                            g = prep_gen(hd + 1, bufs[(hd + 1) % 2]) if hd < 7 else None
                            sweep(hd, bufs[hd % 2], both(g))
                            if g is not None:
                                for _ in g:
                                    pass
                        for _ in gg:
                            pass
                    P.barrier()
                    ck("att%d" % pas)

                if own:
                    with contextlib.ExitStack() as SS_:
                        run_ssd(SS_, None)
                    with contextlib.ExitStack() as SAt_:
                        run_att(SAt_, None)
                else:
                    def ssd_scoped(inj):
                        with contextlib.ExitStack() as SS_:
                            run_ssd(SS_, inj)
                    with contextlib.ExitStack() as SAt_:
                        run_att(SAt_, ssd_scoped)
                P.barrier()

        with contextlib.ExitStack() as SEF:
            h2T = sb("h2T", [128, 8, 2048], BF16, SEF)
            r_h2T = [Res() for _ in range(16)]
            with contextlib.ExitStack() as SCD:
                mergedT = sb("mergedT", [128, 8, 2048], BF16, SCD)
                r_mg = [Res() for _ in range(8)]
                with contextlib.ExitStack() as SC:
                    ynR_ = Ring(nc, SC, "ynTa", [128, 16, 512], BF16, 2)
                    oTR_ = Ring(nc, SC, "oTa", [128, 8, 512], BF16, 2)
                    wosS = Ring(nc, SC, "wosS", [128, 16, 128], F32, 2)
                    wosB = Ring(nc, SC, "wosB", [128, 16, 128], BF16, 2)
                    woaS = Ring(nc, SC, "woaS", [128, 8, 128], F32, 2)
                    woaB = Ring(nc, SC, "woaB", [128, 8, 128], BF16, 2)
                    gsR = Ring(nc, SC, "gs", [128, 512], BF16, 2)
                    gaR = Ring(nc, SC, "ga", [128, 512], BF16, 2)
                    m1R = Ring(nc, SC, "m1", [128, 512], F32, 2)
                    m2R = Ring(nc, SC, "m2", [128, 512], F32, 2)
                    for gq in range(4):
                        ts = slice(gq * 512, (gq + 1) * 512)
                        ynTa, r_yn = ynR_.get()
                        oTa, r_oTa = oTR_.get()
                        P.dma("sp", ynTa[:], ynT_d[:, :, ts].rearrange("k p t -> p k t"), writes=[r_yn])
                        P.dma("sp", oTa[:], oT_d[:, :, ts].rearrange("k p t -> p k t"), writes=[r_oTa])
                        for f in range(8):
                            ws, r_ws = wosS.get()
                            wb, r_wb = wosB.get()
                            P.dma("sp", ws[:], w_o_ssd[:, f * 128:(f + 1) * 128].rearrange("(c p) n -> p c n", p=128), writes=[r_ws])
                            tt("dve", wb[:], ws[:], ssdn[:].unsqueeze(2).to_broadcast([128, 16, 128]), ALU.mult, [r_ws, r_ssdn], [r_wb])
                            as_, r_as = woaS.get()
                            ab_, r_ab = woaB.get()
                            P.dma("sp", as_[:], w_o_attn[:, f * 128:(f + 1) * 128].rearrange("(c p) n -> p c n", p=128), writes=[r_as])
                            act(ab_[:], as_[:], AF.Copy, [r_as], [r_ab])
                            gs, r_gs = gsR.get()
                            ga, r_ga = gaR.get()
                            P.dma("sp", gs[:], gT_d[f, :, ts], writes=[r_gs])
                            P.dma("sp", ga[:], gT_d[8 + f, :, ts], writes=[r_ga])
                            bkA, rbkA = bank()
                            for kt in range(16):
                                mm(bkA[:, :], wb[:, kt, :], ynTa[:, kt, :], kt == 0, kt == 15, [r_wb, r_yn], [rbkA])
                            bkB, rbkB = bank()
                            for kt in range(8):
                                mm(bkB[:, :], ab_[:, kt, :], oTa[:, kt, :], kt == 0, kt == 7, [r_ab, r_oTa], [rbkB])
                            m1, r_m1 = m1R.get()
                            m2, r_m2 = m2R.get()
                            tt("dve", m1[:], bkA[:, :], gs[:], ALU.mult, [rbkA, r_gs], [r_m1])
                            tt("dve", m2[:], bkB[:, :], ga[:], ALU.mult, [rbkB, r_ga], [r_m2])
                            tt("pool", mergedT[:, f, ts], m1[:], m2[:], ALU.add, [r_m1, r_m2], [r_mg[f]])
                    P.barrier()
                    ck("C", mergedT=mergedT[:])

                with contextlib.ExitStack() as SD:
                    woutB = sb("woutB", [128, 8, 1024], BF16, SD)
                    r_wout = Res()
                    wS = Ring(nc, SD, "woutS", [128, 8, 256], F32, 2)
                    for q in range(4):
                        s_t, s_r = wS.get()
                        P.dma("sp", s_t[:], w_out[:, q * 256:(q + 1) * 256].rearrange("(c p) n -> p c n", p=128), writes=[s_r])
                        cp("dve", woutB[:, :, q * 256:(q + 1) * 256], s_t[:], [s_r], [r_wout])
                    xR = Ring(nc, SD, "xtd", [128, 1024], F32, 2)
                    x1R = Ring(nc, SD, "x1", [128, 1024], F32, 3)
                    tmR = Ring(nc, SD, "tm", [128, 1024], F32, 2)
                    xnR = Ring(nc, SD, "xnd", [128, 1024], BF16, 2)
                    sqj = sb("sqjd", [128, 1024], BF16, SD)
                    r_sqj = Res()
                    statR = Ring(nc, SD, "statd", [128, 8], F32, 2)
                    stat2R = Ring(nc, SD, "stat2d", [128, 4], F32, 2)
                    dpend = []
                    pendA2 = None
                    for t in range(16):
                        tks = slice(t * 128, (t + 1) * 128)
                        xt, r_xt = xR.get()
                        P.dma("sp", xt[:], xin[2048 + t * 128:2048 + (t + 1) * 128, :], writes=[r_xt])
                        bk0, rbk0 = bank()
                        bk1, rbk1 = bank()
                        for kc in range(8):
                            mm(bk0[:, :], mergedT[:, kc, tks], woutB[:, kc, 0:512], kc == 0, kc == 7, [r_mg[kc], r_wout], [rbk0])
                        for kc in range(8):
                            mm(bk1[:, :], mergedT[:, kc, tks], woutB[:, kc, 512:1024], kc == 0, kc == 7, [r_mg[kc], r_wout], [rbk1])
                        while dpend:
                            dpend.pop(0)()
                        stat, r_stat = statR.get()
                        act(sqj[:, 0:512], bk0[:, :], AF.Square, [rbk0], [r_sqj, r_stat], accum=stat[:, 0:1])
                        act(sqj[:, 512:1024], bk1[:, :], AF.Square, [rbk1, r_stat], [r_sqj, r_stat], accum=stat[:, 1:2])
                        tt("dve", stat[:, 2:3], stat[:, 0:1], stat[:, 1:2], ALU.add, [r_stat], [r_stat])
                        act(stat[:, 3:4], stat[:, 2:3], AF.Ln, [r_stat], [r_stat], bias=EPS, scale=1.0 / 1024)
                        act(stat[:, 4:5], stat[:, 3:4], AF.Exp, [r_stat], [r_stat], scale=-0.5)
                        tm, r_tm = tmR.get()
                        stt("dve", tm[:, 0:512], bk0[:, :], stat[:, 4:5], g1n[:, 0:512], ALU.mult, ALU.mult, [rbk0, r_stat, r_g1n], [r_tm])
                        stt("dve", tm[:, 512:1024], bk1[:, :], stat[:, 4:5], g1n[:, 512:1024], ALU.mult, ALU.mult,
                            [rbk1, r_stat, r_g1n], [r_tm])
                        x1, r_x1 = x1R.get()
                        tt("dve", x1[:], tm[:], xt[:], ALU.add, [r_tm, r_xt], [r_x1])
                        P.dma("sp", x1_d[t * 128:(t + 1) * 128, :], x1[:], reads=[r_x1])
                        if pendA2 is not None:
                            px1, pr_x1, pt = pendA2
                            stat2, r_stat2 = stat2R.get()
                            norm_transpose(px1, pr_x1, gsc2, sh2, h2T, pt * 128, r_h2T[pt], sqj, r_sqj, stat2, r_stat2, xnR, defer=dpend)
                        pendA2 = (x1, r_x1, t)
                    px1, pr_x1, pt = pendA2
                    stat2, r_stat2 = stat2R.get()
                    norm_transpose(px1, pr_x1, gsc2, sh2, h2T, pt * 128, r_h2T[pt], sqj, r_sqj, stat2, r_stat2, xnR, defer=dpend)
                    while dpend:
                        dpend.pop(0)()
                    P.barrier()
                    ck("D", h2T=h2T[:])

            with contextlib.ExitStack() as SF:
                wdB = sb("wdB", [128, 22, 1024], BF16, SF)
                r_wd = Res()
                wdS = Ring(nc, SF, "wdS", [128, 1, 1024], F32, 2)
                wdv = w_down.rearrange("(c p) n -> p c n", p=128)
                for q in range(22):
                    s_t, s_r = wdS.get()
                    P.dma("sp", s_t[:], wdv[:, q:q + 1, :], writes=[s_r])
                    cp("dve", wdB[:, q:q + 1, :], s_t[:], [s_r], [r_wd])
                aT = sb("aT", [128, 22, 1024], BF16, SF)
                wgS = Ring(nc, SF, "wgS", [128, 8, 128], F32, 2)
                wgB = Ring(nc, SF, "wgB", [128, 8, 128], BF16, 2)
                wuS = Ring(nc, SF, "wuS", [128, 8, 128], F32, 2)
                wuB = Ring(nc, SF, "wuB", [128, 8, 128], BF16, 2)
                sgR = Ring(nc, SF, "sg", [128, 512], F32, 2)
                x1R = Ring(nc, SF, "x1f", [128, 1024], F32, 2)
                tmR = Ring(nc, SF, "tmf", [128, 1024], F32, 2)
                sqj = sb("sqjf", [128, 1024], BF16, SF)
                r_sqj = Res()
                statR = Ring(nc, SF, "statf", [128, 8], F32, 2)
                outs = []
                for half in range(2):
                    r_aT = [Res() for _ in range(22)]
                    for j in range(22):
                        gs_, r_gs_ = wgS.get()
                        gb_, r_gb_ = wgB.get()
                        us_, r_us_ = wuS.get()
                        ub_, r_ub_ = wuB.get()
                        P.dma("sp", gs_[:], w_gate[:, j * 128:(j + 1) * 128].rearrange("(c p) n -> p c n", p=128), writes=[r_gs_])
                        cp("pool", gb_[:], gs_[:], [r_gs_], [r_gb_])
                        P.dma("sp", us_[:], w_up[:, j * 128:(j + 1) * 128].rearrange("(c p) n -> p c n", p=128), writes=[r_us_])
                        act(ub_[:], us_[:], AF.Copy, [r_us_], [r_ub_])
                        for gq in range(2):
                            ts = slice(half * 1024 + gq * 512, half * 1024 + (gq + 1) * 512)
                            rh = r_h2T[half * 8 + gq * 4:half * 8 + (gq + 1) * 4]
                            bkG, rbkG = bank()
                            bkU, rbkU = bank()
                            for kc in range(8):
                                mm(bkG[:, :], gb_[:, kc, :], h2T[:, kc, ts], kc == 0, kc == 7, [r_gb_] + rh, [rbkG])
                            for kc in range(8):
                                mm(bkU[:, :], ub_[:, kc, :], h2T[:, kc, ts], kc == 0, kc == 7, [r_ub_] + rh, [rbkU])
                            sg, r_sg = sgR.get()
                            act(sg[:], bkG[:, :], AF.Silu, [rbkG], [r_sg])
                            tt("dve", aT[:, j, gq * 512:(gq + 1) * 512], bkU[:, :], sg[:], ALU.mult, [rbkU, r_sg], [r_aT[j]])
                    ck("E%d" % half, aT=aT[:])
                    for tl in range(8):
                        t = half * 8 + tl
                        tks = slice(tl * 128, (tl + 1) * 128)
                        x1, r_x1 = x1R.get()
                        P.dma("sp", x1[:], x1_d[t * 128:(t + 1) * 128, :], writes=[r_x1])
                        bk0, rbk0 = bank()
                        bk1, rbk1 = bank()
                        for j in range(22):
                            mm(bk0[:, :], aT[:, j, tks], wdB[:, j, 0:512], j == 0, j == 21, [r_aT[j], r_wd], [rbk0])
                        for j in range(22):
                            mm(bk1[:, :], aT[:, j, tks], wdB[:, j, 512:1024], j == 0, j == 21, [r_aT[j], r_wd], [rbk1])
                        stat, r_stat = statR.get()
                        act(sqj[:, 0:512], bk0[:, :], AF.Square, [rbk0], [r_sqj, r_stat], accum=stat[:, 0:1])
                        act(sqj[:, 512:1024], bk1[:, :], AF.Square, [rbk1, r_stat], [r_sqj, r_stat], accum=stat[:, 1:2])
                        tt("dve", stat[:, 2:3], stat[:, 0:1], stat[:, 1:2], ALU.add, [r_stat], [r_stat])
                        act(stat[:, 3:4], stat[:, 2:3], AF.Ln, [r_stat], [r_stat], bias=EPS, scale=1.0 / 1024)
                        act(stat[:, 4:5], stat[:, 3:4], AF.Exp, [r_stat], [r_stat], scale=-0.5)
                        tm, r_tm = tmR.get()
                        stt("dve", tm[:, 0:512], bk0[:, :], stat[:, 4:5], g2n[:, 0:512], ALU.mult, ALU.mult, [rbk0, r_stat, r_g2n], [r_tm])
                        stt("dve", tm[:, 512:1024], bk1[:, :], stat[:, 4:5], g2n[:, 512:1024], ALU.mult, ALU.mult,
                            [rbk1, r_stat, r_g2n], [r_tm])
                        tt("dve", x1[:], tm[:], x1[:], ALU.add, [r_tm, r_x1], [r_x1])
                        outs.append(P.dma("sp", out[t * 128:(t + 1) * 128, :], x1[:], reads=[r_x1]))
                    P.barrier()
                    ck("F%d" % half)
                P.emit(final_waits=outs)


_CACHE = {}


def _consts():
    ident = np.eye(128, dtype=np.float32)
    prot = np.zeros((128, 128), np.float32)
    for blk in range(2):
        for i in range(32):
            dlo = blk * 64 + i
            dhi = blk * 64 + i + 32
            prot[dhi, dlo] = -1.0
            prot[dlo, dhi] = 1.0
    k = np.arange(128)[:, None]
    l = np.arange(128)[None, :]
    negm = np.where(l < k, NEG, 0.0).astype(np.float32)
    utri = (k <= l).astype(np.float32)
    sel = np.zeros((8, 8, 128), np.float32)
    for h in range(8):
        sel[h, h, :] = 1.0
    invf = (1.0 / (10000.0 ** (np.arange(0, 64, 2, dtype=np.float32) / 64.0))).astype(np.float32)
    invf = np.tile(invf, 4).reshape(128, 1).astype(np.float32)
    return dict(ident=ident, prot=prot, negm=negm, utri=utri, sel=sel, invf=invf)


def _in_maps(x, c, positions, w_ada, b_ada, norm_pre_mix, norm_post_mix, norm_pre_ffn, norm_post_ffn,
           w_in, conv_w, conv_b, dt_bias, a_log, d_skip, ssd_norm, w_o_ssd,
           lambda_q1, lambda_k1, lambda_q2, lambda_k2, subln, w_o_attn, w_out,
           w_gate, w_up, w_down):
    f32 = lambda a: np.ascontiguousarray(np.asarray(a), dtype=np.float32)
    x = f32(x)
    c = f32(c)
    positions = np.ascontiguousarray(np.asarray(positions), dtype=np.int32)
    col = lambda v, n: np.ascontiguousarray(f32(v).reshape(n, 128).T)
    rep = lambda v: np.ascontiguousarray(np.broadcast_to(f32(v).reshape(1, -1), (128, f32(v).size)))
    b_ada0 = f32(b_ada)[0]
    shared = dict(
        w_ada=f32(w_ada)[0], bada_col=col(b_ada0, 48),
        bada_g=np.ascontiguousarray(np.stack([rep(b_ada0[2048:3072]), rep(b_ada0[5120:6144])], axis=1)),
        npre_mix=col(norm_pre_mix, 8), npre_ffn=col(norm_pre_ffn, 8),
        npost_mix=rep(norm_post_mix), npost_ffn=rep(norm_post_ffn),
        w_in=f32(w_in)[0],
        convw=np.ascontiguousarray(f32(conv_w)[0].reshape(24, 128, 4).transpose(1, 0, 2)),
        convb=col(conv_b, 24),
        dtb=rep(dt_bias), alog=rep(a_log), dskip=rep(d_skip), ssdn=col(ssd_norm, 16),
        w_o_ssd=f32(w_o_ssd)[0],
        lam=np.ascontiguousarray(np.stack([rep(lambda_q1), rep(lambda_k1), rep(lambda_q2), rep(lambda_k2)], axis=1)),
        subln=np.ascontiguousarray(f32(subln).reshape(128, 1)),
        w_o_attn=f32(w_o_attn)[0], w_out=f32(w_out)[0], w_gate=f32(w_gate)[0], w_up=f32(w_up)[0],
        w_down=f32(w_down)[0],
    )
    shared.update(_consts())
    in_maps = []
    for core in range(8):
        b, half = core // 2, core % 2
        m = dict(shared)
        if half == 0:
            xin = np.concatenate([np.zeros((2048, 1024), np.float32), x[b, 0:2048]], axis=0)
            pos = np.concatenate([np.zeros((2048,), np.int32), positions[b, 0:2048]], axis=0)
        else:
            xin = x[b]
            pos = positions[b]
        m["xin"] = np.ascontiguousarray(xin)
        m["pos"] = np.ascontiguousarray(np.broadcast_to(pos.reshape(1, 4096), (128, 4096)))
        m["valid"] = np.full((128, 1), float(half), np.float32)
        m["ccol"] = col(c[b], 8)
        in_maps.append(m)
    return in_maps


def kernel(**inputs):
    in_maps = _in_maps(**inputs)
    if "nc" not in _CACHE:
        _CACHE["nc"] = build()
    res = run_bass_kernel_spmd(_CACHE["nc"], in_maps, core_ids=list(range(8)))
    outp = np.empty((4, 4096, 1024), np.float32)
    for core in range(8):
        b, half = core // 2, core % 2
        outp[b, half * 2048:(half + 1) * 2048] = res.results[core]["out"]
    return outp
```
